# Optimizing a Trainium2 kernel written in Bass

```python
import math
import jax, jax.numpy as jnp
from jax import lax
import numpy as np

D_MODEL = 2048
BATCH = 8
SEQ = 4096
DEPTH = 1

HEAD_DIM = 128
CONV_WIDTH = D_MODEL // 2
CONV_GROUPS = CONV_WIDTH // HEAD_DIM
DIL_PAIRS = ((128, 1), (512, 4), (2048, 16))
N_DIL = len(DIL_PAIRS)
HEADS_PER_DIL = (D_MODEL // 2) // HEAD_DIM
N_PAT_HEADS = N_DIL * HEADS_PER_DIL
ATTN_OUT_WIDTH = HEADS_PER_DIL * HEAD_DIM
MIX_WIDTH = CONV_WIDTH + ATTN_OUT_WIDTH
QKV_WIDTH = N_PAT_HEADS * HEAD_DIM
PROJ_WIDTH = 3 * CONV_WIDTH + 3 * QKV_WIDTH
SPLITS = [CONV_WIDTH, 2 * CONV_WIDTH, 3 * CONV_WIDTH,
          3 * CONV_WIDTH + QKV_WIDTH, 3 * CONV_WIDTH + 2 * QKV_WIDTH]
D_FF = 256 * (-(-(8 * D_MODEL) // (3 * 256)))
N_BUCKETS = 32
MAX_DISTANCE = 2048
BLK = 128
LN_EPS = 1e-5
ALPHA = (2 * DEPTH) ** 0.25
BETA = (8 * DEPTH) ** -0.25

kernel_name = "hybrid_conv_dilated_attn_deepnorm_layer"


def _layernorm(x, g, b):
    xf = x.astype(jnp.float32)
    mu = jnp.mean(xf, axis=-1, keepdims=True)
    var = jnp.mean(jnp.square(xf - mu), axis=-1, keepdims=True)
    return ((xf - mu) * lax.rsqrt(var + LN_EPS) * g.astype(jnp.float32) + b.astype(jnp.float32)).astype(x.dtype)


def _causal_conv3(x, w):
    xp = jnp.pad(x, ((0, 0), (2, 0), (0, 0)))
    return w[0] * xp[:, :-2] + w[1] * xp[:, 1:-1] + w[2] * xp[:, 2:]


def _t5_bucket(dist):
    max_exact = N_BUCKETS // 2
    n = np.maximum(dist, 1).astype(np.float32)
    large = max_exact + (np.log(n / max_exact) / math.log(MAX_DISTANCE / max_exact)
                         * (N_BUCKETS - max_exact)).astype(np.int32)
    large = np.minimum(large, N_BUCKETS - 1)
    return np.where(dist < max_exact, dist, large).astype(np.int32)


def _dilated_window_attention(q, k, v, bias_table, window, dilation):
    b, s, h, dh = q.shape
    n_steps = window // dilation
    L = s // dilation
    nb = -(-L // BLK)
    lp = nb * BLK

    def to_blocks(t):
        t = t.reshape(b, L, dilation, h, dh).transpose(0, 2, 3, 1, 4)
        t = jnp.pad(t, ((0, 0), (0, 0), (0, 0), (0, lp - L), (0, 0)))
        return t.reshape(b, dilation, h, nb, BLK, dh)

    def with_prev(t):
        prev = jnp.pad(t, ((0, 0), (0, 0), (0, 0), (1, 0), (0, 0), (0, 0)))[:, :, :, :-1]
        return jnp.concatenate([prev, t], axis=4)

    qb = to_blocks(q)
    kk = with_prev(to_blocks(k))
    vv = with_prev(to_blocks(v))

    qi = np.arange(BLK)[:, None]
    ki = np.arange(2 * BLK)[None, :]
    steps = qi + BLK - ki
    in_window = (steps >= 0) & (steps <= n_steps)
    bucket = _t5_bucket(np.clip(steps, 0, None) * dilation)
    bias = jnp.take(bias_table, jnp.asarray(bucket), axis=0)
    bias = bias.transpose(2, 0, 1).astype(jnp.float32)
    key_exists = (np.arange(nb)[:, None, None] * BLK - BLK + ki[None]) >= 0
    valid = jnp.asarray(in_window[None] & key_exists)

    scores = jnp.einsum('bdhnqc,bdhnkc->bdhnqk', qb, kk).astype(jnp.float32) * (dh ** -0.5)
    scores = scores + bias[:, None]
    scores = jnp.where(valid, scores, jnp.finfo(jnp.float32).min)
    m = jnp.max(scores, axis=-1, keepdims=True)
    p = jnp.exp(scores - m)
    denom = jnp.sum(p, axis=-1, keepdims=True)
    o = jnp.einsum('bdhnqk,bdhnkc->bdhnqc', p, vv.astype(jnp.float32)) / denom
    lse = (m + jnp.log(denom))

    def from_blocks(t):
        t = t.reshape(b, dilation, h, lp, t.shape[-1])[:, :, :, :L]
        return t.transpose(0, 3, 1, 2, 4).reshape(b, s, h, t.shape[-1])

    return from_blocks(o), from_blocks(lse)[..., 0]


def _token_mixer(x, w_in, conv_w, w_out, rel_bias):
    b, s, _ = x.shape
    proj = x @ w_in
    u, gate_b, gate_c, q, k, v = jnp.split(proj, SPLITS, axis=-1)
    y_conv = gate_b * _causal_conv3(gate_c * u, conv_w)
    q = q.reshape(b, s, N_DIL, HEADS_PER_DIL, HEAD_DIM)
    k = k.reshape(b, s, N_DIL, HEADS_PER_DIL, HEAD_DIM)
    v = v.reshape(b, s, N_DIL, HEADS_PER_DIL, HEAD_DIM)
    outs, lses = [], []
    for g, (window, dilation) in enumerate(DIL_PAIRS):
        o_g, lse_g = _dilated_window_attention(
            q[:, :, g], k[:, :, g], v[:, :, g],
            rel_bias[:, g * HEADS_PER_DIL:(g + 1) * HEADS_PER_DIL], window, dilation)
        outs.append(o_g)
        lses.append(lse_g)
    o_all = jnp.stack(outs, axis=0)
    wts = jax.nn.softmax(jnp.stack(lses, axis=0), axis=0)
    y_attn = jnp.sum(wts[..., None] * o_all, axis=0).reshape(b, s, ATTN_OUT_WIDTH).astype(x.dtype)
    return jnp.concatenate([y_conv, y_attn], axis=-1) @ w_out


def _conv_ffn(h, w_up, conv_w, conv_b, w_down):
    up = _causal_conv3(h @ w_up, conv_w) + conv_b
    a, g = jnp.split(up, 2, axis=-1)
    return (jax.nn.silu(g) * a) @ w_down


def setup_inputs(seed: int = 0) -> dict:
    key = jax.random.key(seed)
    ks = jax.random.split(key, 13)
    f32 = jnp.float32
    col_scale = jnp.concatenate([
        jnp.full((CONV_WIDTH,), BETA, f32),
        jnp.ones((2 * CONV_WIDTH + 2 * QKV_WIDTH,), f32),
        jnp.full((QKV_WIDTH,), BETA, f32)])
    return {
        "x": jax.random.normal(ks[0], (BATCH, SEQ, D_MODEL), f32),
        "w_in": jax.random.normal(ks[1], (DEPTH, D_MODEL, PROJ_WIDTH), f32) * (D_MODEL ** -0.5) * col_scale,
        "conv_mix_w": jax.random.normal(ks[2], (DEPTH, 3, CONV_WIDTH), f32) * (3 ** -0.5),
        "w_out": jax.random.normal(ks[3], (DEPTH, MIX_WIDTH, D_MODEL), f32) * (MIX_WIDTH ** -0.5) * BETA,
        "ln1_g": 1.0 + 0.02 * jax.random.normal(ks[4], (DEPTH, D_MODEL), f32),
        "ln1_b": 0.02 * jax.random.normal(ks[5], (DEPTH, D_MODEL), f32),
        "w_up": jax.random.normal(ks[6], (DEPTH, D_MODEL, 2 * D_FF), f32) * (D_MODEL ** -0.5) * BETA,
        "ffn_conv_w": jax.random.normal(ks[7], (DEPTH, 3, 2 * D_FF), f32) * (3 ** -0.5),
        "ffn_conv_b": 0.02 * jax.random.normal(ks[8], (DEPTH, 2 * D_FF), f32),
        "w_down": jax.random.normal(ks[9], (DEPTH, D_FF, D_MODEL), f32) * (D_FF ** -0.5) * BETA,
        "ln2_g": 1.0 + 0.02 * jax.random.normal(ks[10], (DEPTH, D_MODEL), f32),
        "ln2_b": 0.02 * jax.random.normal(ks[11], (DEPTH, D_MODEL), f32),
        "rel_bias": 0.2 * jax.random.normal(ks[12], (N_BUCKETS, N_PAT_HEADS), f32),
    }


def reference(x, w_in, conv_mix_w, w_out, ln1_g, ln1_b, w_up, ffn_conv_w, ffn_conv_b,
              w_down, ln2_g, ln2_b, rel_bias):
    for layer in range(DEPTH):
        mix = _token_mixer(x, w_in[layer], conv_mix_w[layer], w_out[layer], rel_bias)
        x = _layernorm(ALPHA * x + mix, ln1_g[layer], ln1_b[layer])
        ffn = _conv_ffn(x, w_up[layer], ffn_conv_w[layer], ffn_conv_b[layer], w_down[layer])
        x = _layernorm(ALPHA * x + ffn, ln2_g[layer], ln2_b[layer])
    return x
```

```python
import math
from contextlib import ExitStack

import numpy as np
import concourse.bass as bass
import concourse.mybir as mybir
from concourse.bass_utils import run_bass_kernel_spmd

F32 = mybir.dt.float32
BF16 = mybir.dt.bfloat16
AF = mybir.ActivationFunctionType
ALU = mybir.AluOpType

S = 4096
D = 2048
KC = 16
HALF = 2048
DFF = 5632
NFB = 44
PROJ = 12288
ALPHA = float(2.0 ** 0.25)
SCALE = float(128 ** -0.5)
LN_EPS = 1e-5
DILS = (1, 4, 16)
MASKV = -30000.0

ENGS = ("pe", "act", "dve", "pool", "sp")


class Op:
    __slots__ = ("eng", "fn", "deps", "inc", "cnt", "dsem", "dcnt")

    def __init__(self, eng, fn, deps, dsem):
        self.eng = eng
        self.fn = fn
        self.deps = deps
        self.inc = False
        self.cnt = 0
        self.dsem = dsem
        self.dcnt = 0


def _flat(deps, out):
    for d in deps:
        if d is None:
            continue
        if isinstance(d, (list, tuple)):
            _flat(d, out)
        else:
            out.append(d)


class Prog:
    def __init__(self):
        self.q = {e: [] for e in ENGS}
        self.dcount = {}
        self.finals = []

    def add(self, eng, fn, deps=(), dsem=None):
        dl = []
        _flat(deps, dl)
        op = Op(eng, fn, dl, dsem)
        if dsem is not None:
            self.dcount[dsem] = self.dcount.get(dsem, 0) + 16
            op.dcnt = self.dcount[dsem]
        self.q[eng].append(op)
        return op

    def dsem_names(self):
        return sorted(self.dcount.keys())

    def emit(self, block, esems, dsems):
        for e in ENGS:
            for op in self.q[e]:
                for d in op.deps:
                    if d.dsem is None and not (d.eng == "pe" and op.eng == "pe"):
                        d.inc = True
        for e in ENGS:
            c = 0
            for op in self.q[e]:
                if op.dsem is None and op.inc:
                    c += 1
                    op.cnt = c
        finals = self.finals

        def run(ename, eng):
            waited = {}
            for op in self.q[ename]:
                need = {}
                for d in op.deps:
                    if d.dsem is not None:
                        key = ("d", d.dsem)
                        val = d.dcnt
                    else:
                        if d.eng == "pe" and ename == "pe":
                            continue
                        key = ("e", d.eng)
                        val = d.cnt
                    if val > need.get(key, 0):
                        need[key] = val
                for key, val in need.items():
                    if val > waited.get(key, 0):
                        sem = dsems[key[1]] if key[0] == "d" else esems[key[1]]
                        eng.wait_ge(sem, val)
                        waited[key] = val
                ins = op.fn(eng)
                if op.dsem is not None:
                    ins.then_inc(dsems[op.dsem], 16)
                elif op.inc:
                    ins.then_inc(esems[ename], 1)
            if ename == "sp":
                for d in finals:
                    eng.wait_ge(dsems[d.dsem], d.dcnt)

        @block.tensor
        def _(eng):
            run("pe", eng)

        @block.scalar
        def _(eng):
            run("act", eng)

        @block.vector
        def _(eng):
            run("dve", eng)

        @block.gpsimd
        def _(eng):
            run("pool", eng)

        @block.sync
        def _(eng):
            run("sp", eng)


def build_nc(phases="0ABCDE", dbg=False):
    nc = bass.Bass("TRN2", target_bir_lowering=False)

    def din(name, shape):
        return nc.dram_tensor(name, list(shape), F32, kind="ExternalInput").ap()

    skind = "ExternalOutput" if dbg else "Internal"

    def dscr(name, shape, dt):
        return nc.dram_tensor(name, list(shape), dt, kind=skind).ap()

    x = din("x", [S, D])
    w_in = din("w_in", [D, PROJ])
    conv_mix_w = din("conv_mix_w", [3, 1024])
    w_out = din("w_out", [D, D])
    ln1_g = din("ln1_g", [1, D])
    ln1_b = din("ln1_b", [1, D])
    w_up = din("w_up", [D, 2 * DFF])
    ffn_conv_w = din("ffn_conv_w", [3, 2 * DFF])
    ffn_conv_b = din("ffn_conv_b", [1, 2 * DFF])
    w_down = din("w_down", [DFF, D])
    ln2_g = din("ln2_g", [1, D])
    ln2_b = din("ln2_b", [1, D])
    biasT = din("biasT", [24, 128, 256])
    ident = din("ident", [128, 128])
    out = nc.dram_tensor("out", [S, D], F32, kind="ExternalOutput").ap()

    qkv_s = dscr("qkv_s", [3, 24, 128, S], BF16)
    yT_s = dscr("yT_s", [16, 128, S], BF16)
    h_s = dscr("h_s", [S, D], F32)
    hT_s = dscr("hT_s", [16, 128, S], BF16)
    actT_s = dscr("actT_s", [NFB, 128, S], BF16)
    part_s = dscr("part_s", [S, D], F32)
    wout_b = dscr("wout_b", [D, D], BF16)
    wdown_b = dscr("wdown_b", [DFF, D], BF16)

    P = Prog()
    with ExitStack() as es:
        NWORDS = 52800
        big = es.enter_context(nc.sbuf_tensor("big", [128, NWORDS], F32))
        banks = [es.enter_context(nc.psum_tensor("ps%d" % i, [128, 512], F32)) for i in range(8)]

        def carve(off, shape, dt):
            n = int(np.prod(shape[1:]))
            nb = n * (2 if dt == BF16 else 4)
            assert off % 4 == 0 and nb % 4 == 0
            assert off + nb <= NWORDS * 4, (off, nb)
            a = big[:, off // 4:(off + nb) // 4]
            if dt == BF16:
                a = a.bitcast(BF16)
            if len(shape) == 3:
                a = a.rearrange("p (a b) -> p a b", a=shape[1])
            return a

        class Alloc:
            def __init__(self, base):
                self.off = base

            def get(self, shape, dt):
                n = int(np.prod(shape[1:])) * (2 if dt == BF16 else 4)
                n = (n + 31) // 32 * 32
                a = carve(self.off, shape, dt)
                self.off += n
                return a

        G = Alloc(0)
        ident_f = G.get([128, 128], F32)
        ident_b = G.get([128, 128], BF16)
        ones_b = G.get([128, 128], BF16)
        cw = G.get([128, 3, 8], F32)
        halo_t = G.get([128, 8, 2], F32)
        fcw = G.get([128, 3, 88], F32)
        fcb = G.get([128, 88], F32)
        halo_f = G.get([128, 88, 2], F32)
        lnst = G.get([128, 3, 24], F32)
        lnmv = G.get([128, 3, 8], F32)
        eps_t = G.get([128, 8], F32)
        BASE = (G.off + 63) // 64 * 64

        ld_id = P.add("sp", lambda e: e.dma_start(out=ident_f, in_=ident[:, :]), dsem="c0")
        ld_idb = P.add("pool", lambda e: e.dma_start(out=ident_b, in_=ident[:, :]), dsem="c1")
        ld_cw = [P.add("sp", lambda e, j=j: e.dma_start(
            out=cw[:, j, :], in_=conv_mix_w[j:j + 1, :].rearrange("o (c p) -> p (o c)", p=128),
            allow_slow_non_contiguous=True), dsem="c2") for j in range(3)]
        ld_fcw = [P.add("sp", lambda e, j=j: e.dma_start(
            out=fcw[:, j, :], in_=ffn_conv_w[j:j + 1, :].rearrange("o (c p) -> p (o c)", p=128),
            allow_slow_non_contiguous=True), dsem="c3") for j in range(3)]
        ld_fcb = P.add("sp", lambda e: e.dma_start(out=fcb, in_=ffn_conv_b.rearrange("o (c p) -> p (o c)", p=128),
                                                   allow_slow_non_contiguous=True), dsem="c4")
        mk_ones = P.add("dve", lambda e: e.memset(ones_b, 1.0))
        mk_eps = P.add("dve", lambda e: e.memset(eps_t, LN_EPS))
        consts = [ld_id, ld_idb, ld_cw, ld_fcw, ld_fcb, mk_ones, mk_eps]

        state = {"bar": list(consts), "w_pref": None, "precast_ops": [],
                 "precast": [(w_out, wout_b, r * 128) for r in range(16)] +
                            [(w_down, wdown_b, r * 128) for r in range(NFB)]}

        def barrier_ops():
            return list(state["bar"])

        def set_barrier(ops):
            state["bar"] = [o for o in ops if o is not None]

        def s1_half(tag, aT, aT_ready, w_dram, blocks, epilogue, wbufs, pre, single_of=None, hook=None,
                    pre_load=None, aT_ready_k=None, bank_free_init=None):
            slot_last_mm = [None, None, None]
            bank_free = [[pre] * 4, [pre] * 4] if bank_free_init is None else [list(b) for b in bank_free_init]
            all_tail = []
            lds = {}

            def issue_load(bi):
                col0 = blocks[bi][0]
                slot = bi % 3
                if state["precast"]:
                    wsrc, wdst, r0 = state["precast"].pop(0)
                    state["precast_ops"].append(P.add("pool", lambda e, wsrc=wsrc, wdst=wdst, r0=r0: e.dma_start(
                        out=wdst[r0:r0 + 128, :], in_=wsrc[r0:r0 + 128, :]), deps=[], dsem="pc"))
                lds[bi] = P.add(
                    "pool",
                    lambda e, col0=col0, slot=slot: e.dma_start(
                        out=wbufs[slot], in_=w_dram[:, col0:col0 + 128].rearrange("(k p) n -> p k n", p=128)),
                    deps=[slot_last_mm[slot], pre if pre_load is None else pre_load], dsem="w%d" % slot)

            issue_load(0)
            if len(blocks) > 1:
                issue_load(1)
            for bi, (col0, info) in enumerate(blocks):
                slot = bi % 3
                single = bool(single_of and single_of(info))
                bset = 0 if single else bi % 2
                if bi + 2 < len(blocks):
                    issue_load(bi + 2)
                mm_last = [None] * 4
                if single:
                    order = [(k, n) for n in range(4) for k in range(0, 4)] + \
                            [(k, n) for k in range(4, 12) for n in range(4)] + \
                            [(k, n) for n in range(4) for k in range(12, KC)]
                else:
                    order = [(k, n) for k in range(KC) for n in range(4)]
                for oi, (k, n) in enumerate(order):
                    deps = []
                    if k == 0:
                        deps = [lds[bi], bank_free[bset][n], aT_ready]
                    if bi == 0 and aT_ready_k is not None:
                        deps = deps + [aT_ready_k[k]]
                    mm = P.add(
                        "pe",
                        lambda e, k=k, n=n, slot=slot, bset=bset: e.matmul(
                            banks[bset * 4 + n][:], lhsT=wbufs[slot][:, k, :],
                            rhs=aT[:, k, n * 512:(n + 1) * 512], start=(k == 0), stop=(k == KC - 1)),
                        deps=deps)
                    if k == KC - 1:
                        mm_last[n] = mm
                    if hook is not None and oi % 4 == 3:
                        hook(bi, oi // 4)
                slot_last_mm[slot] = mm
                readers, tails = epilogue(info, bset, mm_last)
                bank_free[bset] = readers
                if single:
                    bank_free[1] = readers
                all_tail.extend(tails)
            state["last_banks"] = bank_free
            return all_tail + [slot_last_mm[0], slot_last_mm[1], slot_last_mm[2]]

        def phase0A(hf, fuse=False):
            A = Alloc(BASE)
            xT = A.get([128, KC, HALF], BF16)
            wbufs = [A.get([128, KC, 128], BF16) for _ in range(3)]
            stg = [A.get([128, HALF], BF16) for _ in range(2)]
            reg0 = A.off
            ut = [A.get([128, HALF + 2], F32) for _ in range(2)]
            acc = [A.get([128, HALF], F32) for _ in range(2)]
            xs = [A.get([128, D], F32) for _ in range(2)]
            pre = barrier_ops()
            t0 = hf * HALF

            xs_free = [pre, pre]
            bank_free = [pre] * 8
            evs = []
            ld = [None] * 16

            def ldx(tt):
                b = tt % 2
                ld[tt] = P.add("sp", lambda e, tt=tt, b=b: e.dma_start(
                    out=xs[b], in_=x[t0 + tt * 128:t0 + (tt + 1) * 128, :]), deps=[xs_free[b]], dsem="xs%d" % b)

            ldx(0)
            for tt in range(16):
                b = tt % 2
                if tt + 1 < 16:
                    if tt >= 1:
                        pass
                    ldx(tt + 1) if tt + 1 < 2 else None
                trl = None
                for kg in range(4):
                    bk = (tt * 4 + kg) % 8
                    for j in range(4):
                        k = kg * 4 + j
                        deps = [ld[tt], bank_free[bk]] if j == 0 else []
                        trl = P.add("pe", lambda e, b=b, k=k, bk=bk, j=j: e.transpose(
                            banks[bk][:, j * 128:(j + 1) * 128], xs[b][:, k * 128:(k + 1) * 128], ident_f),
                            deps=deps)
                    eng = "act" if kg % 2 == 0 else "dve"
                    if eng == "act":
                        ev = P.add("act", lambda e, bk=bk, kg=kg, tt=tt: e.activation(
                            out=xT[:, kg * 4:(kg + 1) * 4, tt * 128:(tt + 1) * 128],
                            in_=banks[bk][:].rearrange("p (a b) -> p a b", a=4), func=AF.Copy), deps=[trl])
                    else:
                        ev = P.add("dve", lambda e, bk=bk, kg=kg, tt=tt: e.tensor_copy(
                            out=xT[:, kg * 4:(kg + 1) * 4, tt * 128:(tt + 1) * 128],
                            in_=banks[bk][:].rearrange("p (a b) -> p a b", a=4)), deps=[trl])
                    bank_free[bk] = ev
                    evs.append(ev)
                xs_free[b] = trl
                if tt + 2 < 16:
                    ldx(tt + 2)
            xT_ready = [evs[-1], evs[-2]]

            blocks = []
            for cb in range(8):
                blocks.append((cb * 128, ("u", cb)))
                blocks.append((2048 + cb * 128, ("c", cb)))
                blocks.append((1024 + cb * 128, ("b", cb)))
            units = [(g, h) for h in range(8) for g in range(3)]
            NU = len(units)
            if fuse:
                for (g, h) in units:
                    for which in (2, 1, 0):
                        blocks.append((3072 + which * 3072 + g * 1024 + h * 128, ("qkv", which, g, h)))
            else:
                for g in range(3):
                    for h in range(8):
                        for which in range(3):
                            blocks.append((3072 + which * 3072 + g * 1024 + h * 128, ("qkv", which, g, h)))
            st = {"stg_dma": [pre, pre], "stg_i": 0, "ut_free": [pre, pre], "acc_free": [pre, pre],
                  "conv_done": [None, None], "t_done": [None, None]}

            if fuse:
                AA = Alloc(reg0)
                qkvb = [[AA.get([128, S], BF16) for _ in range(3)] for _ in range(2)]
                vtok = [AA.get([128, 32, 128], BF16) for _ in range(2)]
                bTh = [AA.get([128, 3, 256], F32) for _ in range(2)]
                tmpb = [AA.get([128, 512], F32) for _ in range(2)]
                PTb = [AA.get([128, 512], BF16) for _ in range(3)]
                OD = AA.get([128, 2, S], F32)
                ystg = AA.get([128, S], BF16)
                S_BK = [4, 5]
                OD_BK = 6
                V_BK = 7
                at = {"started": False}
                ld_qkv = {}

                def attn_init():
                    p0 = [P.q["act"][-1], P.q["dve"][-1], P.q["pe"][-1]]
                    at.update({"s_free": [p0, p0], "tmp_free": [p0, p0], "pt_free": [p0, p0, p0],
                               "vt_free": [p0, p0], "vT_free": [p0, p0], "qk_free": [p0, p0], "vbank_free": p0,
                               "od_free": p0, "oacc_free": p0, "ystg_dma": p0, "bT_free": [p0, p0], "ld_bT": {},
                               "v_ev": {}, "ex": {}, "last_qk": {}, "tails": [], "started": True, "evq": {},
                               "half0_done": pre})

                def attn_load(ui, which):
                    if not at["started"]:
                        attn_init()
                    g, h = units[ui]
                    b = ui % 2
                    fr = at["vT_free"][b] if which == 2 else at["qk_free"][b]
                    ld_qkv[(ui, which)] = P.add("sp", lambda e, which=which, g=g, h=h, b=b: e.dma_start(
                        out=qkvb[b][which], in_=qkv_s[which, g * 8 + h]), deps=[fr, at["half0_done"]],
                        dsem="qk%d%d" % (b, which))
                    if which == 2 and g == 0:
                        hb = h % 2
                        at["ld_bT"][h] = P.add("sp", lambda e, h=h, hb=hb: e.dma_start(
                            out=bTh[hb], in_=biasT.rearrange("(g h) p q -> h p g q", g=3)[h]),
                            deps=[at["bT_free"][hb]], dsem="bT%d" % hb)

                def v_round(ui, q4):
                    b = ui % 2
                    vT_ = qkvb[b][2]
                    mmv = None
                    for j in range(4):
                        B = q4 * 4 + j
                        deps = [ld_qkv[(ui, 2)], at["evq"][(ui, 2)], at["vbank_free"], at["vt_free"][b],
                                ld_idb] if j == 0 else []
                        mmv = P.add("pe", lambda e, j=j, B=B, vT_=vT_: e.matmul(
                            banks[V_BK][:, j * 128:(j + 1) * 128], lhsT=vT_[:, B * 128:(B + 1) * 128], rhs=ident_b,
                            start=True, stop=True), deps=deps)
                    dst = vtok[b][:, q4 * 4:(q4 + 1) * 4, :]
                    src = banks[V_BK][:].rearrange("p (a b) -> p a b", a=4)
                    ev = P.add("act", lambda e, dst=dst, src=src: e.activation(out=dst, in_=src, func=AF.Copy),
                               deps=[mmv])
                    at["vbank_free"] = ev
                    at["v_ev"][ui] = ev
                    if q4 == 7:
                        at["vT_free"][b] = mmv

                def qk_pair(ui, p):
                    g, h = units[ui]
                    b = ui % 2
                    nb = 32 // DILS[g]
                    qT, kT = qkvb[b][0], qkvb[b][1]
                    bk = S_BK[p % 2]
                    qk = None
                    for i in range(2):
                        B = 2 * p + i
                        width = 256 if (B % nb) + 1 < nb else 128
                        deps = [ld_qkv[(ui, 0)], ld_qkv[(ui, 1)], at["evq"][(ui, 0)], at["evq"][(ui, 1)],
                                at["s_free"][p % 2]] if i == 0 else []
                        qk = P.add("pe", lambda e, bk=bk, i=i, B=B, width=width, kT=kT, qT=qT: e.matmul(
                            banks[bk][:, i * 256:i * 256 + width], lhsT=kT[:, B * 128:(B + 1) * 128],
                            rhs=qT[:, B * 128:B * 128 + width], start=True, stop=True), deps=deps)
                    at["last_qk"][ui] = qk
                    hb = h % 2
                    t1 = P.add("dve", lambda e, bk=bk, p=p, g=g, hb=hb: e.scalar_tensor_tensor(
                        out=tmpb[p % 2].rearrange("p (a b) -> p a b", a=2),
                        in0=banks[bk][:].rearrange("p (a b) -> p a b", a=2), scalar=SCALE,
                        in1=bTh[hb][:, g:g + 1, :].broadcast_to([128, 2, 256]), op0=ALU.mult, op1=ALU.add),
                        deps=[qk, at["tmp_free"][p % 2], at["ld_bT"][h]])
                    at["s_free"][p % 2] = t1
                    if g == 2 and p == 15:
                        at["bT_free"][hb] = t1
                    ex = P.add("act", lambda e, p=p: e.activation(out=PTb[p % 3], in_=tmpb[p % 2], func=AF.Exp),
                               deps=[t1, at["pt_free"][p % 3]])
                    at["tmp_free"][p % 2] = ex
                    at["ex"][(ui, p)] = ex

                def pv_pair(ui, p):
                    g, h = units[ui]
                    b = ui % 2
                    d = DILS[g]
                    nb = 32 // d
                    pv = None
                    firstmm = True
                    for i in range(2):
                        B = 2 * p + i
                        n = B % nb
                        terms = []
                        if n > 0:
                            if i == 1:
                                terms.append((B - 1, PTb[p % 3][:, 128:256], at["ex"][(ui, p)]))
                            else:
                                terms.append((B - 1, PTb[(p - 1) % 3][:, 384:512], at["ex"][(ui, p - 1)]))
                        terms.append((B, PTb[p % 3][:, i * 256:i * 256 + 128], at["ex"][(ui, p)]))
                        for kind, cb_ in (("v", 0), ("1", 256)):
                            for ti_, (Bk, rhs_, exop) in enumerate(terms):
                                deps = [exop, at["v_ev"][ui]]
                                if firstmm:
                                    deps.append(at["od_free"])
                                    firstmm = False
                                lhs = vtok[b][:, Bk, :] if kind == "v" else ones_b
                                pv = P.add("pe", lambda e, cb_=cb_, i=i, lhs=lhs, rhs_=rhs_, ti_=ti_, nt=len(terms): e.matmul(
                                    banks[OD_BK][:, cb_ + i * 128:cb_ + (i + 1) * 128], lhsT=lhs, rhs=rhs_,
                                    start=(ti_ == 0), stop=(ti_ == nt - 1)), deps=deps)
                    at["pt_free"][(p - 1) % 3] = pv
                    if p == 15:
                        at["pt_free"][p % 3] = pv
                        at["vt_free"][b] = pv
                        at["qk_free"][b] = at["last_qk"][ui]
                    B0 = 2 * p
                    start = (B0 // nb) + d * 128 * (B0 % nb)
                    dv = OD[:, :, start:start + d * 255 + 1:d]
                    sv_ = banks[OD_BK][:].rearrange("p (a b) -> p a b", a=2)
                    if g == 0:
                        evo = P.add("dve", lambda e, dv=dv, sv_=sv_: e.tensor_copy(out=dv, in_=sv_),
                                    deps=[pv, at["oacc_free"]])
                    else:
                        evo = P.add("dve", lambda e, dv=dv, sv_=sv_: e.tensor_tensor(
                            out=dv, in0=sv_, in1=dv, op=ALU.add), deps=[pv, at["oacc_free"]])
                    at["od_free"] = evo
                    if g == 2 and p == 15:
                        l1 = P.add("act", lambda e: e.activation(out=OD[:, 1, :], in_=OD[:, 1, :], func=AF.Ln),
                                   deps=[evo])
                        rc = P.add("act", lambda e: e.activation(out=OD[:, 1, :], in_=OD[:, 1, :], func=AF.Exp,
                                                                 scale=-1.0), deps=[l1])
                        nm = P.add("dve", lambda e: e.tensor_tensor(out=ystg, in0=OD[:, 0, :], in1=OD[:, 1, :],
                                                                    op=ALU.mult), deps=[rc, at["ystg_dma"]])
                        at["oacc_free"] = nm
                        at["ystg_dma"] = P.add("sp", lambda e, h=h: e.dma_start(out=yT_s[8 + h], in_=ystg),
                                               deps=[nm], dsem="ystg")
                        at["tails"].append(at["ystg_dma"])

                def attn_step(ui, si):
                    if si == 0:
                        qk_pair(ui, 0)
                    elif si == 1:
                        qk_pair(ui, 1)
                    else:
                        p = si - 2
                        pv_pair(ui, p)
                        if p + 2 < 16:
                            qk_pair(ui, p + 2)

                NCONV = 24
                PV_SLOT = {0: 0, 2: 1}
                for p_ in range(16):
                    PV_SLOT[4 + (11 * p_) // 4] = 2 + p_

                def hook(bi, kg):
                    if bi < NCONV:
                        return
                    ui, wi = divmod(bi - NCONV, 3)
                    if wi == 0 and kg == 0:
                        for which in (2, 1, 0):
                            attn_load(ui, which)
                    if wi == 2 and kg % 2 == 0:
                        v_round(ui, kg // 2)
                    if ui >= 1:
                        sl = wi * 16 + kg
                        if sl in PV_SLOT:
                            attn_step(ui - 1, PV_SLOT[sl])
            else:
                hook = None

            def epi(info, bset, mm_last):
                kind = info[0]
                readers = []
                tails = []
                if kind == "qkv" and fuse:
                    _, which, g, h = info
                    d = DILS[g]
                    ui = units.index((g, h))
                    b = ui % 2
                    sv = qkvb[b][which].rearrange("p (r j) -> p r j", r=d)
                    o0 = HALF // d
                    for n in range(4):
                        src = banks[bset * 4 + n][:].rearrange("p (j r) -> p r j", r=d)
                        dst = sv[:, :, o0 + n * (512 // d):o0 + (n + 1) * (512 // d)]
                        if n % 2 == 0:
                            ev = P.add("act", lambda e, src=src, dst=dst: e.activation(out=dst, in_=src, func=AF.Copy),
                                       deps=[mm_last[n], ld_qkv[(ui, which)]])
                        else:
                            ev = P.add("dve", lambda e, src=src, dst=dst: e.tensor_copy(out=dst, in_=src),
                                       deps=[mm_last[n], ld_qkv[(ui, which)]])
                        readers.append(ev)
                    at["evq"][(ui, which)] = list(readers)
                elif kind == "qkv":
                    _, which, g, h = info
                    d = DILS[g]
                    si = st["stg_i"] % 2
                    st["stg_i"] += 1
                    sv = stg[si].rearrange("p (r j) -> p r j", r=d)
                    for n in range(4):
                        src = banks[bset * 4 + n][:].rearrange("p (j r) -> p r j", r=d)
                        dst = sv[:, :, n * (512 // d):(n + 1) * (512 // d)]
                        if n % 2 == 0:
                            ev = P.add("act", lambda e, src=src, dst=dst: e.activation(out=dst, in_=src, func=AF.Copy),
                                       deps=[mm_last[n], st["stg_dma"][si]])
                        else:
                            ev = P.add("dve", lambda e, src=src, dst=dst: e.tensor_copy(out=dst, in_=src),
                                       deps=[mm_last[n], st["stg_dma"][si]])
                        readers.append(ev)
                    dstd = qkv_s[which, g * 8 + h].rearrange("p (r j) -> p r j", r=d)[
                        :, :, hf * (HALF // d):(hf + 1) * (HALF // d)]
                    dm = P.add("sp", lambda e, sv=sv, dstd=dstd: e.dma_start(out=dstd, in_=sv),
                               deps=readers, dsem="stg%d" % si)
                    st["stg_dma"][si] = dm
                    tails.append(dm)
                elif kind == "u":
                    cb = info[1]
                    ui = cb % 2
                    if hf == 0:
                        hop = P.add("dve", lambda e, ui=ui: e.memset(ut[ui][:, 0:2], 0.0), deps=[st["ut_free"][ui]])
                    else:
                        hop = P.add("dve", lambda e, ui=ui, cb=cb: e.tensor_copy(out=ut[ui][:, 0:2], in_=halo_t[:, cb, :]),
                                    deps=[st["ut_free"][ui]])
                    for n in range(4):
                        dst = ut[ui][:, 2 + n * 512:2 + (n + 1) * 512]
                        if n % 2 == 0:
                            ev = P.add("act", lambda e, dst=dst, bk=bset * 4 + n: e.activation(
                                out=dst, in_=banks[bk][:], func=AF.Copy), deps=[mm_last[n], st["ut_free"][ui]])
                        else:
                            ev = P.add("dve", lambda e, dst=dst, bk=bset * 4 + n: e.tensor_copy(
                                out=dst, in_=banks[bk][:]), deps=[mm_last[n], st["ut_free"][ui]])
                        readers.append(ev)
                    st["u_done"] = readers + [hop]
                elif kind == "c":
                    cb = info[1]
                    ui = cb % 2
                    for n in range(4):
                        dst = ut[ui][:, 2 + n * 512:2 + (n + 1) * 512]
                        ev = P.add("dve", lambda e, dst=dst, bk=bset * 4 + n: e.tensor_tensor(
                            out=dst, in0=banks[bk][:], in1=dst, op=ALU.mult), deps=[mm_last[n], st["u_done"]])
                        readers.append(ev)
                    tdone = readers[-1]
                    if hf == 0:
                        hs = P.add("dve", lambda e, ui=ui, cb=cb: e.tensor_copy(
                            out=halo_t[:, cb, :], in_=ut[ui][:, HALF:HALF + 2]), deps=[tdone])
                        tails.append(hs)
                    a1 = P.add("act", lambda e, ui=ui, cb=cb: e.activation(
                        out=acc[ui], in_=ut[ui][:, 2:HALF + 2], func=AF.Copy, scale=cw[:, 2, cb:cb + 1]),
                        deps=[tdone, st["acc_free"][ui], ld_cw])
                    a2 = P.add("dve", lambda e, ui=ui, cb=cb: e.scalar_tensor_tensor(
                        out=acc[ui], in0=ut[ui][:, 1:HALF + 1], scalar=cw[:, 1, cb:cb + 1], in1=acc[ui],
                        op0=ALU.mult, op1=ALU.add), deps=[a1, tdone])
                    a3 = P.add("dve", lambda e, ui=ui, cb=cb: e.scalar_tensor_tensor(
                        out=acc[ui], in0=ut[ui][:, 0:HALF], scalar=cw[:, 0, cb:cb + 1], in1=acc[ui],
                        op0=ALU.mult, op1=ALU.add), deps=[a2])
                    st["ut_free"][ui] = a3
                    st["conv_done"][ui] = a3
                elif kind == "b":
                    cb = info[1]
                    ui = cb % 2
                    si = st["stg_i"] % 2
                    st["stg_i"] += 1
                    for n in range(4):
                        dst = stg[si][:, n * 512:(n + 1) * 512]
                        ev = P.add("dve", lambda e, dst=dst, bk=bset * 4 + n, ui=ui, n=n: e.tensor_tensor(
                            out=dst, in0=banks[bk][:], in1=acc[ui][:, n * 512:(n + 1) * 512], op=ALU.mult),
                            deps=[mm_last[n], st["conv_done"][ui], st["stg_dma"][si]])
                        readers.append(ev)
                    st["acc_free"][ui] = readers[-1]
                    dm = P.add("sp", lambda e, si=si, cb=cb: e.dma_start(
                        out=yT_s[cb, :, t0:t0 + HALF], in_=stg[si]), deps=readers, dsem="stg%d" % si)
                    st["stg_dma"][si] = dm
                    tails.append(dm)
                return readers, tails

            if fuse:
                tail = s1_half("A%d" % hf, xT, xT_ready, w_in, blocks, epi, wbufs, pre,
                               single_of=lambda info: info[0] == "qkv", hook=hook)
                if "C" in phases:
                    wres_c = carve(BASE, [128, KC, D], BF16)
                    pfc = []
                    for kk in range(0, KC, 4):
                        pfc.append(P.add("sp", lambda e, kk=kk: e.dma_start(
                            out=wres_c[:, kk:kk + 4, :],
                            in_=wout_b[kk * 128:(kk + 4) * 128, :].rearrange("(k p) n -> p k n", p=128)),
                            deps=[o for o in tail[-3:] if o is not None] + state["precast_ops"],
                            dsem="wresc%d" % (kk // 4)))
                    state["w_pref"] = pfc
                for si in range(18):
                    attn_step(NU - 1, si)
                tail = tail + at["tails"][-1:]
            else:
                tail = s1_half("A%d" % hf, xT, xT_ready, w_in, blocks, epi, wbufs, pre)
            set_barrier(tail + [P.q["act"][-1], P.q["dve"][-1], P.q["pe"][-1]] + state["precast_ops"][-1:])

        FUSE = "B" in phases and "A" in phases
        if "A" in phases:
            phase0A(0)
            phase0A(1, fuse=FUSE)

        def phaseB():
            A = Alloc(BASE)
            qkvb = [[A.get([128, S], BF16) for _ in range(3)] for _ in range(2)]
            bT = A.get([128, 24, 256], F32)
            vtok = [A.get([128, 32, 128], BF16) for _ in range(2)]
            tmp = [A.get([128, 256], F32) for _ in range(3)]
            PT = [A.get([128, 256], BF16) for _ in range(6)]
            Oacc = A.get([128, S], F32)
            Dacc = A.get([128, S], F32)
            ystg = A.get([128, S], BF16)
            pre = barrier_ops()
            ld_bT = P.add("sp", lambda e: e.dma_start(out=bT, in_=biasT.rearrange("h p q -> p h q")),
                          deps=pre, dsem="c5")
            units = [(g, h) for h in range(8) for g in range(3)]
            NU = len(units)
            buf_free = [pre, pre]
            lds = {}

            def load_unit(ui):
                g, h = units[ui]
                b = ui % 2
                ops = []
                for which in range(3):
                    ops.append(P.add("sp", lambda e, which=which, g=g, h=h, b=b: e.dma_start(
                        out=qkvb[b][which], in_=qkv_s[which, g * 8 + h]), deps=[buf_free[b]], dsem="qk%d" % b))
                lds[ui] = ops

            S_BK = [0, 1, 2]
            O_BK = [3, 4]
            D_BK = [5, 6]
            V_BK = 7
            stt = {"s_free": [pre] * 3, "tmp_free": [pre] * 3, "pt_free": [pre] * 6, "vt_free": [pre, pre],
                   "vbank_free": pre, "oacc_free": pre, "ystg_dma": pre}
            o_free = {3: pre, 4: pre, 5: pre, 6: pre}
            v_evs = {}

            def v_round(ui_, q4):
                b_ = ui_ % 2
                vT_ = qkvb[b_][2]
                mmv = None
                for j in range(4):
                    B = q4 * 4 + j
                    deps = [lds[ui_], stt["vbank_free"], stt["vt_free"][b_], ld_idb] if j == 0 else []
                    mmv = P.add("pe", lambda e, j=j, B=B, vT_=vT_: e.matmul(
                        banks[V_BK][:, j * 128:(j + 1) * 128], lhsT=vT_[:, B * 128:(B + 1) * 128], rhs=ident_b,
                        start=True, stop=True), deps=deps)
                dst = vtok[b_][:, q4 * 4:(q4 + 1) * 4, :]
                src = banks[V_BK][:].rearrange("p (a b) -> p a b", a=4)
                ev = P.add("act", lambda e, dst=dst, src=src: e.activation(out=dst, in_=src, func=AF.Copy),
                           deps=[mmv])
                stt["vbank_free"] = ev
                v_evs.setdefault(ui_, []).append(ev)

            load_unit(0)
            for q4 in range(8):
                v_round(0, q4)
            last_norm = None
            for ui, (g, h) in enumerate(units):
                b = ui % 2
                d = DILS[g]
                nb = 32 // d
                if ui + 1 < NU:
                    load_unit(ui + 1)
                qT, kT, vT = qkvb[b]
                v_ready = v_evs[ui][-1]
                pt_ops = {}

                def issue_qk(B):
                    n = B % nb
                    width = 256 if n + 1 < nb else 128
                    ss = B % 3
                    bk = S_BK[ss]
                    qk = P.add("pe", lambda e, bk=bk, B=B, width=width, kT=kT, qT=qT: e.matmul(
                        banks[bk][:, 0:width], lhsT=kT[:, B * 128:(B + 1) * 128],
                        rhs=qT[:, B * 128:B * 128 + width], start=True, stop=True),
                        deps=[lds[ui], stt["s_free"][ss]])
                    ti = B % 3
                    t1 = P.add("dve", lambda e, bk=bk, width=width, ti=ti, g=g, h=h: e.scalar_tensor_tensor(
                        out=tmp[ti][:, 0:width], in0=banks[bk][:, 0:width], scalar=SCALE,
                        in1=bT[:, g * 8 + h, 0:width], op0=ALU.mult, op1=ALU.add),
                        deps=[qk, stt["tmp_free"][ti], ld_bT])
                    stt["s_free"][ss] = t1
                    pi = B % 6
                    ex = P.add("act", lambda e, ti=ti, pi=pi, width=width: e.activation(
                        out=PT[pi][:, 0:width], in_=tmp[ti][:, 0:width], func=AF.Exp),
                        deps=[t1, stt["pt_free"][pi]])
                    stt["tmp_free"][ti] = ex
                    pt_ops[B] = (ex, pi)

                issue_qk(0)
                issue_qk(1)
                evo = None
                pv = None
                for B in range(32):
                    if B + 2 < 32:
                        issue_qk(B + 2)
                    if ui + 1 < NU and B >= 16 and B % 2 == 0:
                        v_round(ui + 1, (B - 16) // 2)
                    n = B % nb
                    j4 = B % 4
                    grp = B // 4
                    ob = O_BK[grp % 2]
                    db = D_BK[grp % 2]
                    ex_c, pi_c = pt_ops[B]
                    terms = []
                    if n > 0:
                        ex_p, pi_p = pt_ops[B - 1]
                        terms.append((B - 1, pi_p, 128, ex_p))
                    terms.append((B, pi_c, 0, ex_c))
                    for lhs_kind, bk in (("v", ob), ("1", db)):
                        for ti_, (Bk, pi, c0, exop) in enumerate(terms):
                            deps = [exop, v_ready]
                            if j4 == 0 and ti_ == 0:
                                deps.append(o_free[bk])
                            lhs = vtok[b][:, Bk, :] if lhs_kind == "v" else ones_b
                            pv = P.add("pe", lambda e, bk=bk, j4=j4, lhs=lhs, pi=pi, c0=c0, ti_=ti_, nt=len(terms): e.matmul(
                                banks[bk][:, j4 * 128:(j4 + 1) * 128], lhsT=lhs, rhs=PT[pi][:, c0:c0 + 128],
                                start=(ti_ == 0), stop=(ti_ == nt - 1)), deps=deps)
                    if n > 0:
                        stt["pt_free"][pt_ops[B - 1][1]] = pv
                    if n + 1 == nb:
                        stt["pt_free"][pi_c] = pv
                    if j4 == 3:
                        B0 = B - 3
                        r0 = B0 // nb
                        c0_ = (B0 % nb) * 128
                        nr = 1 if nb >= 4 else 4 // nb
                        ncol = 512 // nr
                        first = (g == 0)
                        for accb, bk in ((Oacc, ob), (Dacc, db)):
                            dv = accb.rearrange("p (j r) -> p r j", r=d)[:, r0:r0 + nr, c0_:c0_ + ncol]
                            sv_ = banks[bk][:].rearrange("p (a b) -> p a b", a=nr)
                            if first and accb is Oacc:
                                evo = P.add("act", lambda e, dv=dv, sv_=sv_: e.activation(
                                    out=dv, in_=sv_, func=AF.Copy), deps=[pv, stt["oacc_free"]])
                            elif first:
                                evo = P.add("dve", lambda e, dv=dv, sv_=sv_: e.tensor_copy(
                                    out=dv, in_=sv_), deps=[pv, stt["oacc_free"]])
                            else:
                                evo = P.add("dve", lambda e, dv=dv, sv_=sv_: e.tensor_tensor(
                                    out=dv, in0=sv_, in1=dv, op=ALU.add), deps=[pv, stt["oacc_free"]])
                            o_free[bk] = evo
                buf_free[b] = pv
                stt["vt_free"][b] = pv
                if g == 2:
                    rc = P.add("dve", lambda e: e.reciprocal(out=Dacc, in_=Dacc), deps=[evo, P.q["act"][-1]])
                    nm = P.add("dve", lambda e: e.tensor_tensor(out=ystg, in0=Oacc, in1=Dacc, op=ALU.mult),
                               deps=[rc, stt["ystg_dma"]])
                    stt["oacc_free"] = nm
                    stt["ystg_dma"] = P.add("sp", lambda e, h=h: e.dma_start(out=yT_s[8 + h], in_=ystg), deps=[nm],
                                            dsem="ystg")
                    last_norm = stt["ystg_dma"]
            set_barrier([last_norm, P.q["act"][-1], P.q["dve"][-1], P.q["pe"][-1]])

        if "B" in phases and not FUSE:
            phaseB()

        def s2_phase(tag, K, w_bf, k0, aT_dram, res_dram, res_scale, gvec, bvec, mode, out_dram, nsets,
                     wres_first=False, prefetch_w=0):
            A = Alloc(BASE)
            NZ = 3
            if wres_first:
                wres = A.get([128, K, D], BF16)
            aT = [A.get([128, K, 512], BF16) for _ in range(2)]
            rt = [A.get([128, D], F32) for _ in range(2)]
            zt = [A.get([128, D], F32) for _ in range(NZ)]
            if mode != "E1":
                gam = A.get([128, D], F32)
                bet = A.get([128, D], F32)
            if mode == "C":
                hTs = A.get([128, KC, 512], BF16)
            if not wres_first:
                wres = A.get([128, K, D], BF16)
            pre = barrier_ops()
            wl = []
            kk = prefetch_w
            while kk < K:
                k2 = min(K, kk + 4)
                wl.append(P.add("sp", lambda e, kk=kk, k2=k2: e.dma_start(
                    out=wres[:, kk:k2, :],
                    in_=w_bf[(k0 + kk) * 128:(k0 + k2) * 128, :].rearrange("(k p) n -> p k n", p=128)),
                    deps=pre, dsem="wres%d" % (len(wl) % 6)))
                kk = k2
            w_ready = [wl[-1]] if wl else []
            w_pref_ops = []
            if state.get("w_pref") is not None:
                w_pref_ops = [state["w_pref"]]
                w_ready.append(state["w_pref"])
                state["w_pref"] = None

            def w_dep(ti, k):
                if ti > 0:
                    return w_ready
                if k < prefetch_w:
                    return w_pref_ops
                return [wl[(k - prefetch_w) // 4]]
            cst = []
            if mode != "E1":
                cst.append(P.add("sp", lambda e: e.dma_start(out=gam, in_=gvec.partition_broadcast(128)),
                                 deps=pre, dsem="c6"))
                cst.append(P.add("sp", lambda e: e.dma_start(out=bet, in_=bvec.partition_broadcast(128)),
                                 deps=pre, dsem="c7"))
                cst.append(mk_eps)
            aT_free = [pre, pre]
            rt_free = [pre, pre]
            zt_free = [pre] * NZ
            bank_free = [[pre] * 4 for _ in range(2)]
            tr_bank_free = {4: pre, 5: pre, 6: pre, 7: pre}
            stv_free = [pre] * NZ
            hs = {"hTs_free": pre, "hT_evs": []}
            lda = {}
            ldr = {}
            T = {}
            tails = []

            def load_a(gi):
                b = gi % 2
                lda[gi] = P.add("sp", lambda e, gi=gi, b=b: e.dma_start(
                    out=aT[b], in_=aT_dram[k0:k0 + K, :, gi * 512:(gi + 1) * 512].rearrange("k p t -> p k t")),
                    deps=[aT_free[b]], dsem="aT%d" % b)

            def load_r(ti):
                b = ti % 2
                ldr[ti] = P.add("sp", lambda e, ti=ti, b=b: e.dma_start(
                    out=rt[b], in_=res_dram[ti * 128:(ti + 1) * 128, :]), deps=[rt_free[b]], dsem="rt%d" % b)

            def stage0(ti):
                gi = ti // 4
                tl = ti % 4
                ab = gi % 2
                rb = ti % 2
                zb = ti % NZ
                bset = ti % nsets
                if tl == 0 and gi + 1 < 8:
                    load_a(gi + 1)
                if ti + 1 < 32:
                    load_r(ti + 1)
                mm_last = [None] * 4
                for k in range(K):
                    for n in range(4):
                        deps = [lda[gi], w_dep(ti, k), bank_free[bset][n]] if k == 0 else \
                            ([w_dep(ti, k)] if (ti == 0 and n == 0) else [])
                        mm = P.add("pe", lambda e, k=k, n=n, ab=ab, tl=tl, bset=bset: e.matmul(
                            banks[bset * 4 + n][:], lhsT=aT[ab][:, k, tl * 128:(tl + 1) * 128],
                            rhs=wres[:, k, n * 512:(n + 1) * 512], start=(k == 0), stop=(k == K - 1)), deps=deps)
                        if k == K - 1:
                            mm_last[n] = mm
                if tl == 3:
                    aT_free[ab] = mm_last[3]
                st = {"mm_last": mm_last[3]}
                zops = []
                for n in range(4):
                    zsl = zt[zb][:, n * 512:(n + 1) * 512]
                    rsl = rt[rb][:, n * 512:(n + 1) * 512]
                    if res_scale is not None:
                        zo = P.add("dve", lambda e, zsl=zsl, rsl=rsl, bk=bset * 4 + n: e.scalar_tensor_tensor(
                            out=zsl, in0=rsl, scalar=res_scale, in1=banks[bk][:], op0=ALU.mult, op1=ALU.add),
                            deps=[mm_last[n], ldr[ti], zt_free[zb]])
                    else:
                        zo = P.add("dve", lambda e, zsl=zsl, rsl=rsl, bk=bset * 4 + n: e.tensor_tensor(
                            out=zsl, in0=banks[bk][:], in1=rsl, op=ALU.add),
                            deps=[mm_last[n], ldr[ti], zt_free[zb]])
                    zops.append(zo)
                bank_free[bset] = zops
                rt_free[rb] = zops[-1]
                st["z"] = zops[-1]
                if mode != "E1":
                    stv = lnst[:, zb, :]
                    mv = lnmv[:, zb, :]
                    sops = None
                    for c in range(4):
                        sops = P.add("dve", lambda e, c=c, stv=stv, zb=zb: e.bn_stats(
                            out=stv[:, c * 6:(c + 1) * 6], in_=zt[zb][:, c * 512:(c + 1) * 512]),
                            deps=[zops[-1], stv_free[zb]])
                    ag = P.add("dve", lambda e, stv=stv, mv=mv: e.bn_aggr(out=mv[:, 0:2], in_=stv), deps=[sops])
                    st["sd"] = P.add("act", lambda e, mv=mv: e.activation(
                        out=mv[:, 2:3], in_=mv[:, 1:2], func=AF.Sqrt, bias=eps_t[:, 0:1]), deps=[ag] + cst)
                T[ti] = st

            def stage1(ti):
                st = T[ti]
                zb = ti % NZ
                if mode == "E1":
                    return
                mv = lnmv[:, zb, :]
                rs = P.add("dve", lambda e, mv=mv: e.reciprocal(out=mv[:, 3:4], in_=mv[:, 2:3]), deps=[st["sd"]])
                nm = P.add("dve", lambda e, mv=mv: e.tensor_scalar(
                    out=mv[:, 4:5], in0=mv[:, 0:1], scalar1=mv[:, 3:4], scalar2=-1.0,
                    op0=ALU.mult, op1=ALU.mult), deps=[rs])
                zn = P.add("act", lambda e, mv=mv, zb=zb: e.activation(
                    out=zt[zb], in_=zt[zb], func=AF.Identity, scale=mv[:, 3:4], bias=mv[:, 4:5]), deps=[nm])
                stv_free[zb] = zn
                m1 = P.add("dve", lambda e, zb=zb: e.tensor_tensor(out=zt[zb], in0=zt[zb], in1=gam, op=ALU.mult),
                           deps=[zn] + cst)
                m2 = P.add("pool", lambda e, zb=zb: e.tensor_tensor(out=zt[zb], in0=zt[zb], in1=bet, op=ALU.add),
                           deps=[m1] + cst)
                st["ln"] = m2

            def stage2(ti):
                st = T[ti]
                zb = ti % NZ
                gi = ti // 4
                tl = ti % 4
                src_dep = st["z"] if mode == "E1" else st["ln"]
                dm = P.add("sp", lambda e, ti=ti, zb=zb: e.dma_start(
                    out=out_dram[ti * 128:(ti + 1) * 128, :], in_=zt[zb]), deps=[src_dep], dsem="zo%d" % zb)
                tails.append(dm)
                if mode != "C":
                    zt_free[zb] = dm
                    return
                trl = None
                for kg in range(4):
                    bk = 4 + kg
                    for j in range(4):
                        k = kg * 4 + j
                        deps = [src_dep, tr_bank_free[bk], ld_id] if j == 0 else []
                        trl = P.add("pe", lambda e, bk=bk, j=j, k=k, zb=zb: e.transpose(
                            banks[bk][:, j * 128:(j + 1) * 128], zt[zb][:, k * 128:(k + 1) * 128], ident_f), deps=deps)
                    dst = hTs[:, kg * 4:(kg + 1) * 4, tl * 128:(tl + 1) * 128]
                    src = banks[bk][:].rearrange("p (a b) -> p a b", a=4)
                    ev = P.add("act", lambda e, dst=dst, src=src: e.activation(out=dst, in_=src, func=AF.Copy),
                               deps=[trl, hs["hTs_free"]])
                    tr_bank_free[bk] = ev
                    hs["hT_evs"].append(ev)
                zt_free[zb] = [dm, trl]
                if tl == 3:
                    hd = P.add("sp", lambda e, gi=gi: e.dma_start(
                        out=hT_s[:, :, gi * 512:(gi + 1) * 512].rearrange("k p t -> p k t"), in_=hTs),
                        deps=[hs["hT_evs"][-1]], dsem="hTs")
                    hs["hTs_free"] = hd
                    tails.append(hd)

            load_a(0)
            load_r(0)
            for it in range(32 + 2):
                if it < 32:
                    stage0(it)
                if 0 <= it - 1 < 32:
                    stage1(it - 1)
                if 0 <= it - 2 < 32:
                    stage2(it - 2)
            set_barrier(tails[-4:] + [P.q["act"][-1], P.q["dve"][-1], P.q["pe"][-1], P.q["pool"][-1]])
            return tails

        if "C" in phases:
            s2_phase("C", 16, wout_b, 0, yT_s, x, ALPHA, ln1_g, ln1_b, "C", h_s, 1, wres_first=True,
                     prefetch_w=(16 if FUSE else 0))

        def phaseD_all():
            A = Alloc(BASE)
            hTb = [A.get([128, KC, HALF], BF16) for _ in range(2)]
            wbufs = [A.get([128, KC, 128], BF16) for _ in range(3)]
            raw = [A.get([128, HALF + 2], F32) for _ in range(2)]
            aconv = [A.get([128, HALF], F32) for _ in range(2)]
            gconv = [A.get([128, HALF], F32) for _ in range(2)]
            astg = [A.get([128, HALF], BF16) for _ in range(2)]
            pre = barrier_ops()
            hT_k = []
            hT1_all = []
            for hf in range(2):
                for k in range(KC):
                    ldh = P.add("sp", lambda e, k=k, hf=hf: e.dma_start(
                        out=hTb[hf][:, k, :], in_=hT_s[k, :, hf * HALF:(hf + 1) * HALF]), deps=pre,
                        dsem=("hT0_%d" % k) if hf == 0 else "hT1")
                    if hf == 0:
                        hT_k.append(ldh)
                    else:
                        hT1_all = [ldh]
            st = {"raw_free": [pre, pre], "i": 0, "aconv_free": [pre, pre], "gconv_free": [pre, pre],
                  "astg_dma": [pre, pre], "aconv_done": [None, None], "hf": 0}

            def epi(info, bset, mm_last):
                hf = st["hf"]
                t0 = hf * HALF
                kind, cb = info
                ch = cb if kind == "a" else NFB + cb
                ri = st["i"] % 2
                st["i"] += 1
                ci = cb % 2
                readers = []
                tails = []
                if hf == 0:
                    hop = P.add("dve", lambda e, ri=ri: e.memset(raw[ri][:, 0:2], 0.0), deps=[st["raw_free"][ri]])
                else:
                    hop = P.add("dve", lambda e, ri=ri, ch=ch: e.tensor_copy(out=raw[ri][:, 0:2], in_=halo_f[:, ch, :]),
                                deps=[st["raw_free"][ri]])
                for n in range(4):
                    dst = raw[ri][:, 2 + n * 512:2 + (n + 1) * 512]
                    ev = P.add("act", lambda e, dst=dst, bk=bset * 4 + n: e.activation(
                        out=dst, in_=banks[bk][:], func=AF.Copy), deps=[mm_last[n], st["raw_free"][ri]])
                    readers.append(ev)
                rdone = [readers[-1], hop]
                if hf == 0:
                    hs = P.add("dve", lambda e, ri=ri, ch=ch: e.tensor_copy(
                        out=halo_f[:, ch, :], in_=raw[ri][:, HALF:HALF + 2]), deps=rdone)
                    tails.append(hs)
                dstb = aconv[ci] if kind == "a" else gconv[ci]
                dfree = st["aconv_free"][ci] if kind == "a" else st["gconv_free"][ci]
                c1 = P.add("dve", lambda e, ri=ri, ch=ch, dstb=dstb: e.tensor_scalar(
                    out=dstb, in0=raw[ri][:, 2:HALF + 2], scalar1=fcw[:, 2, ch:ch + 1], scalar2=fcb[:, ch:ch + 1],
                    op0=ALU.mult, op1=ALU.add), deps=rdone + [dfree, ld_fcw, ld_fcb])
                c2 = P.add("dve", lambda e, ri=ri, ch=ch, dstb=dstb: e.scalar_tensor_tensor(
                    out=dstb, in0=raw[ri][:, 1:HALF + 1], scalar=fcw[:, 1, ch:ch + 1], in1=dstb,
                    op0=ALU.mult, op1=ALU.add), deps=[c1])
                c3 = P.add("dve", lambda e, ri=ri, ch=ch, dstb=dstb: e.scalar_tensor_tensor(
                    out=dstb, in0=raw[ri][:, 0:HALF], scalar=fcw[:, 0, ch:ch + 1], in1=dstb,
                    op0=ALU.mult, op1=ALU.add), deps=[c2])
                st["raw_free"][ri] = c3
                if kind == "a":
                    st["aconv_done"][ci] = c3
                else:
                    sg = P.add("act", lambda e, ci=ci: e.activation(out=gconv[ci], in_=gconv[ci], func=AF.Silu),
                               deps=[c3])
                    mu = P.add("pool", lambda e, ci=ci: e.tensor_tensor(
                        out=astg[ci], in0=gconv[ci], in1=aconv[ci], op=ALU.mult),
                        deps=[sg, st["aconv_done"][ci], st["astg_dma"][ci]])
                    st["aconv_free"][ci] = mu
                    st["gconv_free"][ci] = mu
                    dm = P.add("sp", lambda e, ci=ci, cb=cb, t0=t0: e.dma_start(
                        out=actT_s[cb, :, t0:t0 + HALF], in_=astg[ci]), deps=[mu], dsem="astg%d" % ci)
                    st["astg_dma"][ci] = dm
                    tails.append(dm)
                return readers, tails

            blocks = []
            for cb in range(NFB):
                blocks.append((cb * 128, ("a", cb)))
                blocks.append((DFF + cb * 128, ("g", cb)))
            tail0 = s1_half("D0", hTb[0], None, w_up, blocks, epi, wbufs, pre, aT_ready_k=hT_k)
            pf_deps = [o for o in tail0[-3:] if o is not None]
            wres_e = carve(BASE, [128, 22, D], BF16)
            pf = None
            for kk in range(0, 16, 4):
                pf = P.add("sp", lambda e, kk=kk: e.dma_start(
                    out=wres_e[:, kk:kk + 4, :],
                    in_=wdown_b[kk * 128:(kk + 4) * 128, :].rearrange("(k p) n -> p k n", p=128)),
                    deps=pf_deps + state["precast_ops"][-1:], dsem="wres")
            state["w_pref"] = pf
            st["hf"] = 1
            pre1 = [o for o in tail0 if o is not None] + [P.q["act"][-1], P.q["dve"][-1], P.q["pe"][-1], P.q["pool"][-1]]
            tail1 = s1_half("D1", hTb[1], hT1_all, w_up, blocks, epi, wbufs, pre1, pre_load=pf_deps,
                            bank_free_init=state["last_banks"])
            set_barrier(tail1 + [P.q["act"][-1], P.q["dve"][-1], P.q["pe"][-1], P.q["pool"][-1]])

        if "D" in phases:
            phaseD_all()

        if "E" in phases:
            s2_phase("E1", 22, wdown_b, 0, actT_s, h_s, ALPHA, None, None, "E1", part_s, 2, wres_first=True,
                     prefetch_w=(16 if "D" in phases else 0))
            tails = s2_phase("E2", 22, wdown_b, 22, actT_s, part_s, None, ln2_g, ln2_b, "E2", out, 2, wres_first=True)

        P.finals = [o for o in state["bar"] if o is not None and o.dsem is not None]
        last_c = [o for o in state["bar"] if o is not None and o.dsem is None]
        if last_c:
            fin = P.add("sp", lambda e: e.nop(), deps=last_c)
        esems = {e: es.enter_context(nc.semaphore("s_" + e)) for e in ENGS}
        dsems = {n: es.enter_context(nc.semaphore("d_" + n)) for n in P.dsem_names()}
        block = es.enter_context(nc.Block())
        P.emit(block, esems, dsems)
    return nc


def _t5_bucket(dist):
    max_exact = 16
    n = np.maximum(dist, 1).astype(np.float32)
    large = max_exact + (np.log(n / max_exact) / math.log(2048 / max_exact) * (32 - max_exact)).astype(np.int32)
    large = np.minimum(large, 31)
    return np.where(dist < max_exact, dist, large).astype(np.int32)


def _bias_tables(rel_bias):
    rel_bias = np.asarray(rel_bias, dtype=np.float32)
    k = np.arange(128)[:, None]
    qq = np.arange(256)[None, :]
    dist = qq - k
    valid = (dist >= 0) & (dist <= 128)
    outp = np.full((24, 128, 256), MASKV, dtype=np.float32)
    for g, d in enumerate(DILS):
        bucket = _t5_bucket(np.clip(dist, 0, None) * d)
        for h in range(8):
            col = rel_bias[:, g * 8 + h]
            t = col[bucket]
            outp[g * 8 + h][valid] = t[valid]
    return outp


_NC_CACHE = {}


def _get_nc(phases="0ABCDE", dbg=False):
    key = (phases, dbg)
    if key not in _NC_CACHE:
        _NC_CACHE[key] = build_nc(phases, dbg)
    return _NC_CACHE[key]


def make_in_maps(inputs, ncores=8):
    f = lambda a: np.ascontiguousarray(np.asarray(a, dtype=np.float32))
    x = f(inputs["x"])
    shared = {
        "w_in": f(inputs["w_in"])[0], "conv_mix_w": f(inputs["conv_mix_w"])[0], "w_out": f(inputs["w_out"])[0],
        "ln1_g": f(inputs["ln1_g"]), "ln1_b": f(inputs["ln1_b"]), "w_up": f(inputs["w_up"])[0],
        "ffn_conv_w": f(inputs["ffn_conv_w"])[0], "ffn_conv_b": f(inputs["ffn_conv_b"]),
        "w_down": f(inputs["w_down"])[0], "ln2_g": f(inputs["ln2_g"]), "ln2_b": f(inputs["ln2_b"]),
        "biasT": _bias_tables(inputs["rel_bias"]), "ident": np.eye(128, dtype=np.float32),
    }
    maps = []
    for c in range(ncores):
        m = dict(shared)
        m["x"] = np.ascontiguousarray(x[c])
        maps.append(m)
    return maps


def kernel(**inputs):
    nc = _get_nc()
    in_maps = make_in_maps(inputs, 8)
    res = run_bass_kernel_spmd(nc, in_maps, core_ids=list(range(8)))
    return np.stack([np.asarray(r["out"], dtype=np.float32) for r in res.results], axis=0)
```

```python
import math
from contextlib import ExitStack

import numpy as np
import concourse.bass as bass
import concourse.mybir as mybir
from concourse.bass_utils import run_bass_kernel_spmd

F32 = mybir.dt.float32
BF16 = mybir.dt.bfloat16
AF = mybir.ActivationFunctionType
ALU = mybir.AluOpType

S = 4096
D = 2048
KC = 16
HALF = 2048
DFF = 5632
NFB = 44
PROJ = 12288
ALPHA = float(2.0 ** 0.25)
SCALE = float(128 ** -0.5)
LN_EPS = 1e-5
DILS = (1, 4, 16)
MASKV = -30000.0

ENGS = ("pe", "act", "dve", "pool", "sp")


class Op:
    __slots__ = ("eng", "fn", "deps", "inc", "cnt", "dsem", "dcnt")

    def __init__(self, eng, fn, deps, dsem):
        self.eng = eng
        self.fn = fn
        self.deps = deps
        self.inc = False
        self.cnt = 0
        self.dsem = dsem
        self.dcnt = 0


def _flat(deps, out):
    for d in deps:
        if d is None:
            continue
        if isinstance(d, (list, tuple)):
            _flat(d, out)
        else:
            out.append(d)


class Prog:
    def __init__(self):
        self.q = {e: [] for e in ENGS}
        self.dcount = {}
        self.finals = []

    def add(self, eng, fn, deps=(), dsem=None):
        dl = []
        _flat(deps, dl)
        op = Op(eng, fn, dl, dsem)
        if dsem is not None:
            self.dcount[dsem] = self.dcount.get(dsem, 0) + 16
            op.dcnt = self.dcount[dsem]
        self.q[eng].append(op)
        return op

    def dsem_names(self):
        return sorted(self.dcount.keys())

    def emit(self, block, esems, dsems):
        for e in ENGS:
            for op in self.q[e]:
                for d in op.deps:
                    if d.dsem is None and not (d.eng == "pe" and op.eng == "pe"):
                        d.inc = True
        for e in ENGS:
            c = 0
            for op in self.q[e]:
                if op.dsem is None and op.inc:
                    c += 1
                    op.cnt = c
        finals = self.finals

        def run(ename, eng):
            waited = {}
            for op in self.q[ename]:
                need = {}
                for d in op.deps:
                    if d.dsem is not None:
                        key = ("d", d.dsem)
                        val = d.dcnt
                    else:
                        if d.eng == "pe" and ename == "pe":
                            continue
                        key = ("e", d.eng)
                        val = d.cnt
                    if val > need.get(key, 0):
                        need[key] = val
                for key, val in need.items():
                    if val > waited.get(key, 0):
                        sem = dsems[key[1]] if key[0] == "d" else esems[key[1]]
                        eng.wait_ge(sem, val)
                        waited[key] = val
                ins = op.fn(eng)
                if op.dsem is not None:
                    ins.then_inc(dsems[op.dsem], 16)
                elif op.inc:
                    ins.then_inc(esems[ename], 1)
            if ename == "sp":
                for d in finals:
                    eng.wait_ge(dsems[d.dsem], d.dcnt)

        @block.tensor
        def _(eng):
            run("pe", eng)

        @block.scalar
        def _(eng):
            run("act", eng)

        @block.vector
        def _(eng):
            run("dve", eng)

        @block.gpsimd
        def _(eng):
            run("pool", eng)

        @block.sync
        def _(eng):
            run("sp", eng)


def build_nc(phases="0ABCDE", dbg=False):
    nc = bass.Bass("TRN2", target_bir_lowering=False)

    def din(name, shape):
        return nc.dram_tensor(name, list(shape), F32, kind="ExternalInput").ap()

    skind = "ExternalOutput" if dbg else "Internal"

    def dscr(name, shape, dt):
        return nc.dram_tensor(name, list(shape), dt, kind=skind).ap()

    x = din("x", [S, D])
    w_in = din("w_in", [D, PROJ])
    conv_mix_w = din("conv_mix_w", [3, 1024])
    w_out = din("w_out", [D, D])
    ln1_g = din("ln1_g", [1, D])
    ln1_b = din("ln1_b", [1, D])
    w_up = din("w_up", [D, 2 * DFF])
    ffn_conv_w = din("ffn_conv_w", [3, 2 * DFF])
    ffn_conv_b = din("ffn_conv_b", [1, 2 * DFF])
    w_down = din("w_down", [DFF, D])
    ln2_g = din("ln2_g", [1, D])
    ln2_b = din("ln2_b", [1, D])
    biasT = din("biasT", [24, 128, 256])
    ident = din("ident", [128, 128])
    out = nc.dram_tensor("out", [S, D], F32, kind="ExternalOutput").ap()

    qkv_s = dscr("qkv_s", [3, 24, 128, S], BF16)
    yT_s = dscr("yT_s", [16, 128, S], BF16)
    h_s = dscr("h_s", [S, D], F32)
    hT_s = dscr("hT_s", [16, 128, S], BF16)
    actT_s = dscr("actT_s", [NFB, 128, S], BF16)
    part_s = dscr("part_s", [S, D], F32)
    wout_b = dscr("wout_b", [D, D], BF16)
    wdown_b = dscr("wdown_b", [DFF, D], BF16)

    P = Prog()
    with ExitStack() as es:
        NWORDS = 52800
        big = es.enter_context(nc.sbuf_tensor("big", [128, NWORDS], F32))
        banks = [es.enter_context(nc.psum_tensor("ps%d" % i, [128, 512], F32)) for i in range(8)]

        def carve(off, shape, dt):
            n = int(np.prod(shape[1:]))
            nb = n * (2 if dt == BF16 else 4)
            assert off % 4 == 0 and nb % 4 == 0
            assert off + nb <= NWORDS * 4, (off, nb)
            a = big[:, off // 4:(off + nb) // 4]
            if dt == BF16:
                a = a.bitcast(BF16)
            if len(shape) == 3:
                a = a.rearrange("p (a b) -> p a b", a=shape[1])
            return a

        class Alloc:
            def __init__(self, base):
                self.off = base

            def get(self, shape, dt):
                n = int(np.prod(shape[1:])) * (2 if dt == BF16 else 4)
                n = (n + 31) // 32 * 32
                a = carve(self.off, shape, dt)
                self.off += n
                return a

        G = Alloc(0)
        ident_f = G.get([128, 128], F32)
        ident_b = G.get([128, 128], BF16)
        ones_b = G.get([128, 128], BF16)
        cw = G.get([128, 3, 8], F32)
        halo_t = G.get([128, 8, 2], F32)
        fcw = G.get([128, 3, 88], F32)
        fcb = G.get([128, 88], F32)
        halo_f = G.get([128, 88, 2], F32)
        lnst = G.get([128, 3, 24], F32)
        lnmv = G.get([128, 3, 8], F32)
        eps_t = G.get([128, 8], F32)
        BASE = (G.off + 63) // 64 * 64

        ld_id = P.add("sp", lambda e: e.dma_start(out=ident_f, in_=ident[:, :]), dsem="c0")
        ld_idb = P.add("pool", lambda e: e.dma_start(out=ident_b, in_=ident[:, :]), dsem="c1")
        ld_cw = [P.add("sp", lambda e, j=j: e.dma_start(
            out=cw[:, j, :], in_=conv_mix_w[j:j + 1, :].rearrange("o (c p) -> p (o c)", p=128),
            allow_slow_non_contiguous=True), dsem="c2") for j in range(3)]
        ld_fcw = [P.add("sp", lambda e, j=j: e.dma_start(
            out=fcw[:, j, :], in_=ffn_conv_w[j:j + 1, :].rearrange("o (c p) -> p (o c)", p=128),
            allow_slow_non_contiguous=True), dsem="c3") for j in range(3)]
        ld_fcb = P.add("sp", lambda e: e.dma_start(out=fcb, in_=ffn_conv_b.rearrange("o (c p) -> p (o c)", p=128),
                                                   allow_slow_non_contiguous=True), dsem="c4")
        mk_ones = P.add("dve", lambda e: e.memset(ones_b, 1.0))
        mk_eps = P.add("dve", lambda e: e.memset(eps_t, LN_EPS))
        consts = [ld_id, ld_idb, ld_cw, ld_fcw, ld_fcb, mk_ones, mk_eps]

        state = {"bar": list(consts), "w_pref": None, "precast_ops": [],
                 "precast": [(w_out, wout_b, r * 128) for r in range(16)] +
                            [(w_down, wdown_b, r * 128) for r in range(NFB)]}

        def barrier_ops():
            return list(state["bar"])

        def set_barrier(ops):
            state["bar"] = [o for o in ops if o is not None]

        def s1_half(tag, aT, aT_ready, w_dram, blocks, epilogue, wbufs, pre, single_of=None, hook=None,
                    pre_load=None, aT_ready_k=None, bank_free_init=None, slot_off=0, next_cols=None,
                    cont=False):
            slot_last_mm = list(state["last_slots"]) if cont else [None, None, None]
            bank_free = [[pre] * 4, [pre] * 4] if bank_free_init is None else [list(b) for b in bank_free_init]
            all_tail = []
            lds = {}

            def issue_load(bi):
                if bi >= len(blocks):
                    col0 = next_cols[bi - len(blocks)]
                else:
                    col0 = blocks[bi][0]
                slot = (bi + slot_off) % 3
                if state["precast"]:
                    wsrc, wdst, r0 = state["precast"].pop(0)
                    state["precast_ops"].append(P.add("pool", lambda e, wsrc=wsrc, wdst=wdst, r0=r0: e.dma_start(
                        out=wdst[r0:r0 + 128, :], in_=wsrc[r0:r0 + 128, :]), deps=[], dsem="pc"))
                lds[bi] = P.add(
                    "pool",
                    lambda e, col0=col0, slot=slot: e.dma_start(
                        out=wbufs[slot], in_=w_dram[:, col0:col0 + 128].rearrange("(k p) n -> p k n", p=128)),
                    deps=[slot_last_mm[slot], pre if pre_load is None else pre_load], dsem="w%d" % slot)

            if cont and state.get("preloaded"):
                lds.update(state.pop("preloaded"))
            else:
                issue_load(0)
                if len(blocks) > 1:
                    issue_load(1)
            for bi, (col0, info) in enumerate(blocks):
                slot = (bi + slot_off) % 3
                single = bool(single_of and single_of(info))
                bset = 0 if single else bi % 2
                if bi + 2 < len(blocks) or next_cols is not None:
                    issue_load(bi + 2)
                mm_last = [None] * 4
                if single:
                    order = [(k, n) for n in range(4) for k in range(0, 4)] + \
                            [(k, n) for k in range(4, 12) for n in range(4)] + \
                            [(k, n) for n in range(4) for k in range(12, KC)]
                else:
                    order = [(k, n) for k in range(KC) for n in range(4)]
                for oi, (k, n) in enumerate(order):
                    deps = []
                    if k == 0:
                        deps = [lds[bi], bank_free[bset][n], aT_ready]
                    if bi == 0 and aT_ready_k is not None:
                        deps = deps + [aT_ready_k[k]]
                    mm = P.add(
                        "pe",
                        lambda e, k=k, n=n, slot=slot, bset=bset: e.matmul(
                            banks[bset * 4 + n][:], lhsT=wbufs[slot][:, k, :],
                            rhs=aT[:, k, n * 512:(n + 1) * 512], start=(k == 0), stop=(k == KC - 1)),
                        deps=deps)
                    if k == KC - 1:
                        mm_last[n] = mm
                    if hook is not None and oi % 4 == 3:
                        hook(bi, oi // 4)
                slot_last_mm[slot] = mm
                readers, tails = epilogue(info, bset, mm_last)
                bank_free[bset] = readers
                if single:
                    bank_free[1] = readers
                all_tail.extend(tails)
            state["last_banks"] = bank_free
            state["last_slots"] = list(slot_last_mm)
            if next_cols is not None:
                state["preloaded"] = {0: lds[len(blocks)], 1: lds[len(blocks) + 1]}
            return all_tail + [slot_last_mm[0], slot_last_mm[1], slot_last_mm[2]]

        def phase0A(hf, fuse=False):
            A = Alloc(BASE)
            xT = A.get([128, KC, HALF], BF16)
            wbufs = [A.get([128, KC, 128], BF16) for _ in range(3)]
            stg = [A.get([128, HALF], BF16) for _ in range(2)]
            reg0 = A.off
            ut = [A.get([128, HALF + 2], F32) for _ in range(2)]
            acc = [A.get([128, HALF], F32) for _ in range(2)]
            xs = [A.get([128, D], F32) for _ in range(2)]
            pre = barrier_ops()
            t0 = hf * HALF

            xs_free = [pre, pre]
            bank_free = [pre] * 8
            evs = []
            ld = [None] * 16

            def ldx(tt):
                b = tt % 2
                ld[tt] = P.add("sp", lambda e, tt=tt, b=b: e.dma_start(
                    out=xs[b], in_=x[t0 + tt * 128:t0 + (tt + 1) * 128, :]), deps=[xs_free[b]], dsem="xs%d" % b)

            ldx(0)
            for tt in range(16):
                b = tt % 2
                if tt + 1 < 16:
                    if tt >= 1:
                        pass
                    ldx(tt + 1) if tt + 1 < 2 else None
                trl = None
                for kg in range(4):
                    bk = (tt * 4 + kg) % 8
                    for j in range(4):
                        k = kg * 4 + j
                        deps = [ld[tt], bank_free[bk]] if j == 0 else []
                        trl = P.add("pe", lambda e, b=b, k=k, bk=bk, j=j: e.transpose(
                            banks[bk][:, j * 128:(j + 1) * 128], xs[b][:, k * 128:(k + 1) * 128], ident_f),
                            deps=deps)
                    eng = "act" if kg % 2 == 0 else "dve"
                    if eng == "act":
                        ev = P.add("act", lambda e, bk=bk, kg=kg, tt=tt: e.activation(
                            out=xT[:, kg * 4:(kg + 1) * 4, tt * 128:(tt + 1) * 128],
                            in_=banks[bk][:].rearrange("p (a b) -> p a b", a=4), func=AF.Copy), deps=[trl])
                    else:
                        ev = P.add("dve", lambda e, bk=bk, kg=kg, tt=tt: e.tensor_copy(
                            out=xT[:, kg * 4:(kg + 1) * 4, tt * 128:(tt + 1) * 128],
                            in_=banks[bk][:].rearrange("p (a b) -> p a b", a=4)), deps=[trl])
                    bank_free[bk] = ev
                    evs.append(ev)
                xs_free[b] = trl
                if tt + 2 < 16:
                    ldx(tt + 2)
            xT_ready = [evs[-1], evs[-2]]

            blocks = []
            for cb in range(8):
                blocks.append((cb * 128, ("u", cb)))
                blocks.append((2048 + cb * 128, ("c", cb)))
                blocks.append((1024 + cb * 128, ("b", cb)))
            units = [(g, h) for h in range(8) for g in range(3)]
            NU = len(units)
            if fuse:
                for (g, h) in units:
                    for which in (2, 1, 0):
                        blocks.append((3072 + which * 3072 + g * 1024 + h * 128, ("qkv", which, g, h)))
            else:
                for g in range(3):
                    for h in range(8):
                        for which in range(3):
                            blocks.append((3072 + which * 3072 + g * 1024 + h * 128, ("qkv", which, g, h)))
            st = {"stg_dma": [pre, pre], "stg_i": 0, "ut_free": [pre, pre], "acc_free": [pre, pre],
                  "conv_done": [None, None], "t_done": [None, None]}

            if fuse:
                AA = Alloc(reg0)
                qkvb = [[AA.get([128, S], BF16) for _ in range(3)] for _ in range(2)]
                vtok = [AA.get([128, 32, 128], BF16) for _ in range(2)]
                bTh = [AA.get([128, 3, 256], F32) for _ in range(2)]
                tmpb = [AA.get([128, 512], F32) for _ in range(2)]
                PTb = [AA.get([128, 512], BF16) for _ in range(3)]
                OD = AA.get([128, 2, S], F32)
                ystg = AA.get([128, S], BF16)
                S_BK = [4, 5]
                OD_BK = 6
                V_BK = 7
                at = {"started": False}
                ld_qkv = {}

                def attn_init():
                    p0 = [P.q["act"][-1], P.q["dve"][-1], P.q["pe"][-1]]
                    at.update({"s_free": [p0, p0], "tmp_free": [p0, p0], "pt_free": [p0, p0, p0],
                               "vt_free": [p0, p0], "vT_free": [p0, p0], "qk_free": [p0, p0], "vbank_free": p0,
                               "od_free": p0, "oacc_free": p0, "ystg_dma": p0, "bT_free": [p0, p0], "ld_bT": {},
                               "v_ev": {}, "ex": {}, "last_qk": {}, "tails": [], "started": True, "evq": {},
                               "half0_done": pre})

                def attn_load(ui, which):
                    if not at["started"]:
                        attn_init()
                    g, h = units[ui]
                    b = ui % 2
                    fr = at["vT_free"][b] if which == 2 else at["qk_free"][b]
                    ld_qkv[(ui, which)] = P.add("sp", lambda e, which=which, g=g, h=h, b=b: e.dma_start(
                        out=qkvb[b][which], in_=qkv_s[which, g * 8 + h]), deps=[fr, at["half0_done"]],
                        dsem="qk%d%d" % (b, which))
                    if which == 2 and g == 0:
                        hb = h % 2
                        at["ld_bT"][h] = P.add("sp", lambda e, h=h, hb=hb: e.dma_start(
                            out=bTh[hb], in_=biasT.rearrange("(g h) p q -> h p g q", g=3)[h]),
                            deps=[at["bT_free"][hb]], dsem="bT%d" % hb)

                def v_round(ui, q4):
                    b = ui % 2
                    vT_ = qkvb[b][2]
                    mmv = None
                    for j in range(4):
                        B = q4 * 4 + j
                        deps = [ld_qkv[(ui, 2)], at["evq"][(ui, 2)], at["vbank_free"], at["vt_free"][b],
                                ld_idb] if j == 0 else []
                        mmv = P.add("pe", lambda e, j=j, B=B, vT_=vT_: e.matmul(
                            banks[V_BK][:, j * 128:(j + 1) * 128], lhsT=vT_[:, B * 128:(B + 1) * 128], rhs=ident_b,
                            start=True, stop=True), deps=deps)
                    dst = vtok[b][:, q4 * 4:(q4 + 1) * 4, :]
                    src = banks[V_BK][:].rearrange("p (a b) -> p a b", a=4)
                    ev = P.add("act", lambda e, dst=dst, src=src: e.activation(out=dst, in_=src, func=AF.Copy),
                               deps=[mmv])
                    at["vbank_free"] = ev
                    at["v_ev"][ui] = ev
                    if q4 == 7:
                        at["vT_free"][b] = mmv

                def qk_pair(ui, p):
                    g, h = units[ui]
                    b = ui % 2
                    nb = 32 // DILS[g]
                    qT, kT = qkvb[b][0], qkvb[b][1]
                    bk = S_BK[p % 2]
                    qk = None
                    for i in range(2):
                        B = 2 * p + i
                        width = 256 if (B % nb) + 1 < nb else 128
                        deps = [ld_qkv[(ui, 0)], ld_qkv[(ui, 1)], at["evq"][(ui, 0)], at["evq"][(ui, 1)],
                                at["s_free"][p % 2]] if i == 0 else []
                        qk = P.add("pe", lambda e, bk=bk, i=i, B=B, width=width, kT=kT, qT=qT: e.matmul(
                            banks[bk][:, i * 256:i * 256 + width], lhsT=kT[:, B * 128:(B + 1) * 128],
                            rhs=qT[:, B * 128:B * 128 + width], start=True, stop=True), deps=deps)
                    at["last_qk"][ui] = qk
                    hb = h % 2
                    t1 = P.add("dve", lambda e, bk=bk, p=p, g=g, hb=hb: e.scalar_tensor_tensor(
                        out=tmpb[p % 2].rearrange("p (a b) -> p a b", a=2),
                        in0=banks[bk][:].rearrange("p (a b) -> p a b", a=2), scalar=SCALE,
                        in1=bTh[hb][:, g:g + 1, :].broadcast_to([128, 2, 256]), op0=ALU.mult, op1=ALU.add),
                        deps=[qk, at["tmp_free"][p % 2], at["ld_bT"][h]])
                    at["s_free"][p % 2] = t1
                    if g == 2 and p == 15:
                        at["bT_free"][hb] = t1
                    ex = P.add("act", lambda e, p=p: e.activation(out=PTb[p % 3], in_=tmpb[p % 2], func=AF.Exp),
                               deps=[t1, at["pt_free"][p % 3]])
                    at["tmp_free"][p % 2] = ex
                    at["ex"][(ui, p)] = ex

                def pv_pair(ui, p):
                    g, h = units[ui]
                    b = ui % 2
                    d = DILS[g]
                    nb = 32 // d
                    pv = None
                    firstmm = True
                    for i in range(2):
                        B = 2 * p + i
                        n = B % nb
                        terms = []
                        if n > 0:
                            if i == 1:
                                terms.append((B - 1, PTb[p % 3][:, 128:256], at["ex"][(ui, p)]))
                            else:
                                terms.append((B - 1, PTb[(p - 1) % 3][:, 384:512], at["ex"][(ui, p - 1)]))
                        terms.append((B, PTb[p % 3][:, i * 256:i * 256 + 128], at["ex"][(ui, p)]))
                        for kind, cb_ in (("v", 0), ("1", 256)):
                            for ti_, (Bk, rhs_, exop) in enumerate(terms):
                                deps = [exop, at["v_ev"][ui]]
                                if firstmm:
                                    deps.append(at["od_free"])
                                    firstmm = False
                                lhs = vtok[b][:, Bk, :] if kind == "v" else ones_b
                                pv = P.add("pe", lambda e, cb_=cb_, i=i, lhs=lhs, rhs_=rhs_, ti_=ti_, nt=len(terms): e.matmul(
                                    banks[OD_BK][:, cb_ + i * 128:cb_ + (i + 1) * 128], lhsT=lhs, rhs=rhs_,
                                    start=(ti_ == 0), stop=(ti_ == nt - 1)), deps=deps)
                    at["pt_free"][(p - 1) % 3] = pv
                    if p == 15:
                        at["pt_free"][p % 3] = pv
                        at["vt_free"][b] = pv
                        at["qk_free"][b] = at["last_qk"][ui]
                    B0 = 2 * p
                    start = (B0 // nb) + d * 128 * (B0 % nb)
                    dv = OD[:, :, start:start + d * 255 + 1:d]
                    sv_ = banks[OD_BK][:].rearrange("p (a b) -> p a b", a=2)
                    if g == 0:
                        evo = P.add("dve", lambda e, dv=dv, sv_=sv_: e.tensor_copy(out=dv, in_=sv_),
                                    deps=[pv, at["oacc_free"]])
                    else:
                        evo = P.add("dve", lambda e, dv=dv, sv_=sv_: e.tensor_tensor(
                            out=dv, in0=sv_, in1=dv, op=ALU.add), deps=[pv, at["oacc_free"]])
                    at["od_free"] = evo
                    if g == 2 and p == 15:
                        l1 = P.add("act", lambda e: e.activation(out=OD[:, 1, :], in_=OD[:, 1, :], func=AF.Ln),
                                   deps=[evo])
                        rc = P.add("act", lambda e: e.activation(out=OD[:, 1, :], in_=OD[:, 1, :], func=AF.Exp,
                                                                 scale=-1.0), deps=[l1])
                        nm = P.add("dve", lambda e: e.tensor_tensor(out=ystg, in0=OD[:, 0, :], in1=OD[:, 1, :],
                                                                    op=ALU.mult), deps=[rc, at["ystg_dma"]])
                        at["oacc_free"] = nm
                        at["ystg_dma"] = P.add("sp", lambda e, h=h: e.dma_start(out=yT_s[8 + h], in_=ystg),
                                               deps=[nm], dsem="ystg")
                        at["tails"].append(at["ystg_dma"])

                def attn_step(ui, si):
                    if si == 0:
                        qk_pair(ui, 0)
                    elif si == 1:
                        qk_pair(ui, 1)
                    else:
                        p = si - 2
                        pv_pair(ui, p)
                        if p + 2 < 16:
                            qk_pair(ui, p + 2)

                NCONV = 24
                PV_SLOT = {0: 0, 2: 1}
                for p_ in range(16):
                    PV_SLOT[4 + (11 * p_) // 4] = 2 + p_

                def hook(bi, kg):
                    if bi < NCONV:
                        return
                    ui, wi = divmod(bi - NCONV, 3)
                    if wi == 0 and kg == 0:
                        for which in (2, 1, 0):
                            attn_load(ui, which)
                    if wi == 2 and kg % 2 == 0:
                        v_round(ui, kg // 2)
                    if ui >= 1:
                        sl = wi * 16 + kg
                        if sl in PV_SLOT:
                            attn_step(ui - 1, PV_SLOT[sl])
            else:
                hook = None

            def epi(info, bset, mm_last):
                kind = info[0]
                readers = []
                tails = []
                if kind == "qkv" and fuse:
                    _, which, g, h = info
                    d = DILS[g]
                    ui = units.index((g, h))
                    b = ui % 2
                    sv = qkvb[b][which].rearrange("p (r j) -> p r j", r=d)
                    o0 = HALF // d
                    for n in range(4):
                        src = banks[bset * 4 + n][:].rearrange("p (j r) -> p r j", r=d)
                        dst = sv[:, :, o0 + n * (512 // d):o0 + (n + 1) * (512 // d)]
                        if n % 2 == 0:
                            ev = P.add("act", lambda e, src=src, dst=dst: e.activation(out=dst, in_=src, func=AF.Copy),
                                       deps=[mm_last[n], ld_qkv[(ui, which)]])
                        else:
                            ev = P.add("dve", lambda e, src=src, dst=dst: e.tensor_copy(out=dst, in_=src),
                                       deps=[mm_last[n], ld_qkv[(ui, which)]])
                        readers.append(ev)
                    at["evq"][(ui, which)] = list(readers)
                elif kind == "qkv":
                    _, which, g, h = info
                    d = DILS[g]
                    si = st["stg_i"] % 2
                    st["stg_i"] += 1
                    sv = stg[si].rearrange("p (r j) -> p r j", r=d)
                    for n in range(4):
                        src = banks[bset * 4 + n][:].rearrange("p (j r) -> p r j", r=d)
                        dst = sv[:, :, n * (512 // d):(n + 1) * (512 // d)]
                        if n % 2 == 0:
                            ev = P.add("act", lambda e, src=src, dst=dst: e.activation(out=dst, in_=src, func=AF.Copy),
                                       deps=[mm_last[n], st["stg_dma"][si]])
                        else:
                            ev = P.add("dve", lambda e, src=src, dst=dst: e.tensor_copy(out=dst, in_=src),
                                       deps=[mm_last[n], st["stg_dma"][si]])
                        readers.append(ev)
                    dstd = qkv_s[which, g * 8 + h].rearrange("p (r j) -> p r j", r=d)[
                        :, :, hf * (HALF // d):(hf + 1) * (HALF // d)]
                    dm = P.add("sp", lambda e, sv=sv, dstd=dstd: e.dma_start(out=dstd, in_=sv),
                               deps=readers, dsem="stg%d" % si)
                    st["stg_dma"][si] = dm
                    tails.append(dm)
                elif kind == "u":
                    cb = info[1]
                    ui = cb % 2
                    if hf == 0:
                        hop = P.add("dve", lambda e, ui=ui: e.memset(ut[ui][:, 0:2], 0.0), deps=[st["ut_free"][ui]])
                    else:
                        hop = P.add("dve", lambda e, ui=ui, cb=cb: e.tensor_copy(out=ut[ui][:, 0:2], in_=halo_t[:, cb, :]),
                                    deps=[st["ut_free"][ui]])
                    for n in range(4):
                        dst = ut[ui][:, 2 + n * 512:2 + (n + 1) * 512]
                        if n % 2 == 0:
                            ev = P.add("act", lambda e, dst=dst, bk=bset * 4 + n: e.activation(
                                out=dst, in_=banks[bk][:], func=AF.Copy), deps=[mm_last[n], st["ut_free"][ui]])
                        else:
                            ev = P.add("dve", lambda e, dst=dst, bk=bset * 4 + n: e.tensor_copy(
                                out=dst, in_=banks[bk][:]), deps=[mm_last[n], st["ut_free"][ui]])
                        readers.append(ev)
                    st["u_done"] = readers + [hop]
                elif kind == "c":
                    cb = info[1]
                    ui = cb % 2
                    for n in range(4):
                        dst = ut[ui][:, 2 + n * 512:2 + (n + 1) * 512]
                        ev = P.add("dve", lambda e, dst=dst, bk=bset * 4 + n: e.tensor_tensor(
                            out=dst, in0=banks[bk][:], in1=dst, op=ALU.mult), deps=[mm_last[n], st["u_done"]])
                        readers.append(ev)
                    tdone = readers[-1]
                    if hf == 0:
                        hs = P.add("dve", lambda e, ui=ui, cb=cb: e.tensor_copy(
                            out=halo_t[:, cb, :], in_=ut[ui][:, HALF:HALF + 2]), deps=[tdone])
                        tails.append(hs)
                    a1 = P.add("act", lambda e, ui=ui, cb=cb: e.activation(
                        out=acc[ui], in_=ut[ui][:, 2:HALF + 2], func=AF.Copy, scale=cw[:, 2, cb:cb + 1]),
                        deps=[tdone, st["acc_free"][ui], ld_cw])
                    a2 = P.add("dve", lambda e, ui=ui, cb=cb: e.scalar_tensor_tensor(
                        out=acc[ui], in0=ut[ui][:, 1:HALF + 1], scalar=cw[:, 1, cb:cb + 1], in1=acc[ui],
                        op0=ALU.mult, op1=ALU.add), deps=[a1, tdone])
                    a3 = P.add("dve", lambda e, ui=ui, cb=cb: e.scalar_tensor_tensor(
                        out=acc[ui], in0=ut[ui][:, 0:HALF], scalar=cw[:, 0, cb:cb + 1], in1=acc[ui],
                        op0=ALU.mult, op1=ALU.add), deps=[a2])
                    st["ut_free"][ui] = a3
                    st["conv_done"][ui] = a3
                elif kind == "b":
                    cb = info[1]
                    ui = cb % 2
                    si = st["stg_i"] % 2
                    st["stg_i"] += 1
                    for n in range(4):
                        dst = stg[si][:, n * 512:(n + 1) * 512]
                        ev = P.add("dve", lambda e, dst=dst, bk=bset * 4 + n, ui=ui, n=n: e.tensor_tensor(
                            out=dst, in0=banks[bk][:], in1=acc[ui][:, n * 512:(n + 1) * 512], op=ALU.mult),
                            deps=[mm_last[n], st["conv_done"][ui], st["stg_dma"][si]])
                        readers.append(ev)
                    st["acc_free"][ui] = readers[-1]
                    dm = P.add("sp", lambda e, si=si, cb=cb: e.dma_start(
                        out=yT_s[cb, :, t0:t0 + HALF], in_=stg[si]), deps=readers, dsem="stg%d" % si)
                    st["stg_dma"][si] = dm
                    tails.append(dm)
                return readers, tails

            if fuse:
                tail = s1_half("A%d" % hf, xT, xT_ready, w_in, blocks, epi, wbufs, pre,
                               single_of=lambda info: info[0] == "qkv", hook=hook)
                if "C" in phases:
                    wres_c = carve(BASE, [128, KC, D], BF16)
                    pfc = []
                    for kk in range(0, KC, 4):
                        pfc.append(P.add("sp", lambda e, kk=kk: e.dma_start(
                            out=wres_c[:, kk:kk + 4, :],
                            in_=wout_b[kk * 128:(kk + 4) * 128, :].rearrange("(k p) n -> p k n", p=128)),
                            deps=[o for o in tail[-3:] if o is not None] + state["precast_ops"],
                            dsem="wresc%d" % (kk // 4)))
                    state["w_pref"] = pfc
                for si in range(18):
                    attn_step(NU - 1, si)
                tail = tail + at["tails"][-1:]
            else:
                tail = s1_half("A%d" % hf, xT, xT_ready, w_in, blocks, epi, wbufs, pre)
            set_barrier(tail + [P.q["act"][-1], P.q["dve"][-1], P.q["pe"][-1]] + state["precast_ops"][-1:])

        FUSE = "B" in phases and "A" in phases
        if "A" in phases:
            phase0A(0)
            phase0A(1, fuse=FUSE)

        def phaseB():
            A = Alloc(BASE)
            qkvb = [[A.get([128, S], BF16) for _ in range(3)] for _ in range(2)]
            bT = A.get([128, 24, 256], F32)
            vtok = [A.get([128, 32, 128], BF16) for _ in range(2)]
            tmp = [A.get([128, 256], F32) for _ in range(3)]
            PT = [A.get([128, 256], BF16) for _ in range(6)]
            Oacc = A.get([128, S], F32)
            Dacc = A.get([128, S], F32)
            ystg = A.get([128, S], BF16)
            pre = barrier_ops()
            ld_bT = P.add("sp", lambda e: e.dma_start(out=bT, in_=biasT.rearrange("h p q -> p h q")),
                          deps=pre, dsem="c5")
            units = [(g, h) for h in range(8) for g in range(3)]
            NU = len(units)
            buf_free = [pre, pre]
            lds = {}

            def load_unit(ui):
                g, h = units[ui]
                b = ui % 2
                ops = []
                for which in range(3):
                    ops.append(P.add("sp", lambda e, which=which, g=g, h=h, b=b: e.dma_start(
                        out=qkvb[b][which], in_=qkv_s[which, g * 8 + h]), deps=[buf_free[b]], dsem="qk%d" % b))
                lds[ui] = ops

            S_BK = [0, 1, 2]
            O_BK = [3, 4]
            D_BK = [5, 6]
            V_BK = 7
            stt = {"s_free": [pre] * 3, "tmp_free": [pre] * 3, "pt_free": [pre] * 6, "vt_free": [pre, pre],
                   "vbank_free": pre, "oacc_free": pre, "ystg_dma": pre}
            o_free = {3: pre, 4: pre, 5: pre, 6: pre}
            v_evs = {}

            def v_round(ui_, q4):
                b_ = ui_ % 2
                vT_ = qkvb[b_][2]
                mmv = None
                for j in range(4):
                    B = q4 * 4 + j
                    deps = [lds[ui_], stt["vbank_free"], stt["vt_free"][b_], ld_idb] if j == 0 else []
                    mmv = P.add("pe", lambda e, j=j, B=B, vT_=vT_: e.matmul(
                        banks[V_BK][:, j * 128:(j + 1) * 128], lhsT=vT_[:, B * 128:(B + 1) * 128], rhs=ident_b,
                        start=True, stop=True), deps=deps)
                dst = vtok[b_][:, q4 * 4:(q4 + 1) * 4, :]
                src = banks[V_BK][:].rearrange("p (a b) -> p a b", a=4)
                ev = P.add("act", lambda e, dst=dst, src=src: e.activation(out=dst, in_=src, func=AF.Copy),
                           deps=[mmv])
                stt["vbank_free"] = ev
                v_evs.setdefault(ui_, []).append(ev)

            load_unit(0)
            for q4 in range(8):
                v_round(0, q4)
            last_norm = None
            for ui, (g, h) in enumerate(units):
                b = ui % 2
                d = DILS[g]
                nb = 32 // d
                if ui + 1 < NU:
                    load_unit(ui + 1)
                qT, kT, vT = qkvb[b]
                v_ready = v_evs[ui][-1]
                pt_ops = {}

                def issue_qk(B):
                    n = B % nb
                    width = 256 if n + 1 < nb else 128
                    ss = B % 3
                    bk = S_BK[ss]
                    qk = P.add("pe", lambda e, bk=bk, B=B, width=width, kT=kT, qT=qT: e.matmul(
                        banks[bk][:, 0:width], lhsT=kT[:, B * 128:(B + 1) * 128],
                        rhs=qT[:, B * 128:B * 128 + width], start=True, stop=True),
                        deps=[lds[ui], stt["s_free"][ss]])
                    ti = B % 3
                    t1 = P.add("dve", lambda e, bk=bk, width=width, ti=ti, g=g, h=h: e.scalar_tensor_tensor(
                        out=tmp[ti][:, 0:width], in0=banks[bk][:, 0:width], scalar=SCALE,
                        in1=bT[:, g * 8 + h, 0:width], op0=ALU.mult, op1=ALU.add),
                        deps=[qk, stt["tmp_free"][ti], ld_bT])
                    stt["s_free"][ss] = t1
                    pi = B % 6
                    ex = P.add("act", lambda e, ti=ti, pi=pi, width=width: e.activation(
                        out=PT[pi][:, 0:width], in_=tmp[ti][:, 0:width], func=AF.Exp),
                        deps=[t1, stt["pt_free"][pi]])
                    stt["tmp_free"][ti] = ex
                    pt_ops[B] = (ex, pi)

                issue_qk(0)
                issue_qk(1)
                evo = None
                pv = None
                for B in range(32):
                    if B + 2 < 32:
                        issue_qk(B + 2)
                    if ui + 1 < NU and B >= 16 and B % 2 == 0:
                        v_round(ui + 1, (B - 16) // 2)
                    n = B % nb
                    j4 = B % 4
                    grp = B // 4
                    ob = O_BK[grp % 2]
                    db = D_BK[grp % 2]
                    ex_c, pi_c = pt_ops[B]
                    terms = []
                    if n > 0:
                        ex_p, pi_p = pt_ops[B - 1]
                        terms.append((B - 1, pi_p, 128, ex_p))
                    terms.append((B, pi_c, 0, ex_c))
                    for lhs_kind, bk in (("v", ob), ("1", db)):
                        for ti_, (Bk, pi, c0, exop) in enumerate(terms):
                            deps = [exop, v_ready]
                            if j4 == 0 and ti_ == 0:
                                deps.append(o_free[bk])
                            lhs = vtok[b][:, Bk, :] if lhs_kind == "v" else ones_b
                            pv = P.add("pe", lambda e, bk=bk, j4=j4, lhs=lhs, pi=pi, c0=c0, ti_=ti_, nt=len(terms): e.matmul(
                                banks[bk][:, j4 * 128:(j4 + 1) * 128], lhsT=lhs, rhs=PT[pi][:, c0:c0 + 128],
                                start=(ti_ == 0), stop=(ti_ == nt - 1)), deps=deps)
                    if n > 0:
                        stt["pt_free"][pt_ops[B - 1][1]] = pv
                    if n + 1 == nb:
                        stt["pt_free"][pi_c] = pv
                    if j4 == 3:
                        B0 = B - 3
                        r0 = B0 // nb
                        c0_ = (B0 % nb) * 128
                        nr = 1 if nb >= 4 else 4 // nb
                        ncol = 512 // nr
                        first = (g == 0)
                        for accb, bk in ((Oacc, ob), (Dacc, db)):
                            dv = accb.rearrange("p (j r) -> p r j", r=d)[:, r0:r0 + nr, c0_:c0_ + ncol]
                            sv_ = banks[bk][:].rearrange("p (a b) -> p a b", a=nr)
                            if first and accb is Oacc:
                                evo = P.add("act", lambda e, dv=dv, sv_=sv_: e.activation(
                                    out=dv, in_=sv_, func=AF.Copy), deps=[pv, stt["oacc_free"]])
                            elif first:
                                evo = P.add("dve", lambda e, dv=dv, sv_=sv_: e.tensor_copy(
                                    out=dv, in_=sv_), deps=[pv, stt["oacc_free"]])
                            else:
                                evo = P.add("dve", lambda e, dv=dv, sv_=sv_: e.tensor_tensor(
                                    out=dv, in0=sv_, in1=dv, op=ALU.add), deps=[pv, stt["oacc_free"]])
                            o_free[bk] = evo
                buf_free[b] = pv
                stt["vt_free"][b] = pv
                if g == 2:
                    rc = P.add("dve", lambda e: e.reciprocal(out=Dacc, in_=Dacc), deps=[evo, P.q["act"][-1]])
                    nm = P.add("dve", lambda e: e.tensor_tensor(out=ystg, in0=Oacc, in1=Dacc, op=ALU.mult),
                               deps=[rc, stt["ystg_dma"]])
                    stt["oacc_free"] = nm
                    stt["ystg_dma"] = P.add("sp", lambda e, h=h: e.dma_start(out=yT_s[8 + h], in_=ystg), deps=[nm],
                                            dsem="ystg")
                    last_norm = stt["ystg_dma"]
            set_barrier([last_norm, P.q["act"][-1], P.q["dve"][-1], P.q["pe"][-1]])

        if "B" in phases and not FUSE:
            phaseB()

        def s2_phase(tag, K, w_bf, k0, aT_dram, res_dram, res_scale, gvec, bvec, mode, out_dram, nsets,
                     wres_first=False, prefetch_w=0):
            A = Alloc(BASE)
            NZ = 3
            if wres_first:
                wres = A.get([128, K, D], BF16)
            aT = [A.get([128, K, 512], BF16) for _ in range(2)]
            rt = [A.get([128, D], F32) for _ in range(2)]
            zt = [A.get([128, D], F32) for _ in range(NZ)]
            if mode != "E1":
                gam = A.get([128, D], F32)
                bet = A.get([128, D], F32)
            if mode == "C":
                hTs = A.get([128, KC, 512], BF16)
            if not wres_first:
                wres = A.get([128, K, D], BF16)
            pre = barrier_ops()
            wl = []
            kk = prefetch_w
            while kk < K:
                k2 = min(K, kk + 4)
                wl.append(P.add("sp", lambda e, kk=kk, k2=k2: e.dma_start(
                    out=wres[:, kk:k2, :],
                    in_=w_bf[(k0 + kk) * 128:(k0 + k2) * 128, :].rearrange("(k p) n -> p k n", p=128)),
                    deps=pre, dsem="wres%d" % (len(wl) % 6)))
                kk = k2
            w_ready = [wl[-1]] if wl else []
            w_pref_ops = []
            if state.get("w_pref") is not None:
                w_pref_ops = [state["w_pref"]]
                w_ready.append(state["w_pref"])
                state["w_pref"] = None

            def w_dep(ti, k):
                if ti > 0:
                    return w_ready
                if k < prefetch_w:
                    return w_pref_ops
                return [wl[(k - prefetch_w) // 4]]
            cst = []
            if mode != "E1":
                cst.append(P.add("sp", lambda e: e.dma_start(out=gam, in_=gvec.partition_broadcast(128)),
                                 deps=pre, dsem="c6"))
                cst.append(P.add("sp", lambda e: e.dma_start(out=bet, in_=bvec.partition_broadcast(128)),
                                 deps=pre, dsem="c7"))
                cst.append(mk_eps)
            aT_free = [pre, pre]
            rt_free = [pre, pre]
            zt_free = [pre] * NZ
            bank_free = [[pre] * 4 for _ in range(2)]
            tr_bank_free = {4: pre, 5: pre, 6: pre, 7: pre}
            stv_free = [pre] * NZ
            hs = {"hTs_free": pre, "hT_evs": []}
            lda = {}
            ldr = {}
            T = {}
            tails = []

            def load_a(gi):
                b = gi % 2
                lda[gi] = P.add("sp", lambda e, gi=gi, b=b: e.dma_start(
                    out=aT[b], in_=aT_dram[k0:k0 + K, :, gi * 512:(gi + 1) * 512].rearrange("k p t -> p k t")),
                    deps=[aT_free[b]], dsem="aT%d" % b)

            def load_r(ti):
                b = ti % 2
                ldr[ti] = P.add("sp", lambda e, ti=ti, b=b: e.dma_start(
                    out=rt[b], in_=res_dram[ti * 128:(ti + 1) * 128, :]), deps=[rt_free[b]], dsem="rt%d" % b)

            def stage0(ti):
                gi = ti // 4
                tl = ti % 4
                ab = gi % 2
                rb = ti % 2
                zb = ti % NZ
                bset = ti % nsets
                if tl == 0 and gi + 1 < 8:
                    load_a(gi + 1)
                if ti + 1 < 32:
                    load_r(ti + 1)
                mm_last = [None] * 4
                for k in range(K):
                    for n in range(4):
                        deps = [lda[gi], w_dep(ti, k), bank_free[bset][n]] if k == 0 else \
                            ([w_dep(ti, k)] if (ti == 0 and n == 0) else [])
                        mm = P.add("pe", lambda e, k=k, n=n, ab=ab, tl=tl, bset=bset: e.matmul(
                            banks[bset * 4 + n][:], lhsT=aT[ab][:, k, tl * 128:(tl + 1) * 128],
                            rhs=wres[:, k, n * 512:(n + 1) * 512], start=(k == 0), stop=(k == K - 1)), deps=deps)
                        if k == K - 1:
                            mm_last[n] = mm
                if tl == 3:
                    aT_free[ab] = mm_last[3]
                st = {"mm_last": mm_last[3]}
                zops = []
                for n in range(4):
                    zsl = zt[zb][:, n * 512:(n + 1) * 512]
                    rsl = rt[rb][:, n * 512:(n + 1) * 512]
                    if res_scale is not None:
                        zo = P.add("dve", lambda e, zsl=zsl, rsl=rsl, bk=bset * 4 + n: e.scalar_tensor_tensor(
                            out=zsl, in0=rsl, scalar=res_scale, in1=banks[bk][:], op0=ALU.mult, op1=ALU.add),
                            deps=[mm_last[n], ldr[ti], zt_free[zb]])
                    else:
                        zo = P.add("dve", lambda e, zsl=zsl, rsl=rsl, bk=bset * 4 + n: e.tensor_tensor(
                            out=zsl, in0=banks[bk][:], in1=rsl, op=ALU.add),
                            deps=[mm_last[n], ldr[ti], zt_free[zb]])
                    zops.append(zo)
                bank_free[bset] = zops
                rt_free[rb] = zops[-1]
                st["z"] = zops[-1]
                if mode != "E1":
                    stv = lnst[:, zb, :]
                    mv = lnmv[:, zb, :]
                    sops = None
                    for c in range(4):
                        sops = P.add("dve", lambda e, c=c, stv=stv, zb=zb: e.bn_stats(
                            out=stv[:, c * 6:(c + 1) * 6], in_=zt[zb][:, c * 512:(c + 1) * 512]),
                            deps=[zops[-1], stv_free[zb]])
                    ag = P.add("dve", lambda e, stv=stv, mv=mv: e.bn_aggr(out=mv[:, 0:2], in_=stv), deps=[sops])
                    st["sd"] = P.add("act", lambda e, mv=mv: e.activation(
                        out=mv[:, 2:3], in_=mv[:, 1:2], func=AF.Sqrt, bias=eps_t[:, 0:1]), deps=[ag] + cst)
                T[ti] = st

            def stage1(ti):
                st = T[ti]
                zb = ti % NZ
                if mode == "E1":
                    return
                mv = lnmv[:, zb, :]
                rs = P.add("dve", lambda e, mv=mv: e.reciprocal(out=mv[:, 3:4], in_=mv[:, 2:3]), deps=[st["sd"]])
                nm = P.add("dve", lambda e, mv=mv: e.tensor_scalar(
                    out=mv[:, 4:5], in0=mv[:, 0:1], scalar1=mv[:, 3:4], scalar2=-1.0,
                    op0=ALU.mult, op1=ALU.mult), deps=[rs])
                zn = P.add("act", lambda e, mv=mv, zb=zb: e.activation(
                    out=zt[zb], in_=zt[zb], func=AF.Identity, scale=mv[:, 3:4], bias=mv[:, 4:5]), deps=[nm])
                stv_free[zb] = zn
                m1 = P.add("dve", lambda e, zb=zb: e.tensor_tensor(out=zt[zb], in0=zt[zb], in1=gam, op=ALU.mult),
                           deps=[zn] + cst)
                m2 = P.add("pool", lambda e, zb=zb: e.tensor_tensor(out=zt[zb], in0=zt[zb], in1=bet, op=ALU.add),
                           deps=[m1] + cst)
                st["ln"] = m2

            def stage2(ti):
                st = T[ti]
                zb = ti % NZ
                gi = ti // 4
                tl = ti % 4
                src_dep = st["z"] if mode == "E1" else st["ln"]
                dm = P.add("sp", lambda e, ti=ti, zb=zb: e.dma_start(
                    out=out_dram[ti * 128:(ti + 1) * 128, :], in_=zt[zb]), deps=[src_dep], dsem="zo%d" % zb)
                tails.append(dm)
                if mode != "C":
                    zt_free[zb] = dm
                    return
                trl = None
                for kg in range(4):
                    bk = 4 + kg
                    for j in range(4):
                        k = kg * 4 + j
                        deps = [src_dep, tr_bank_free[bk], ld_id] if j == 0 else []
                        trl = P.add("pe", lambda e, bk=bk, j=j, k=k, zb=zb: e.transpose(
                            banks[bk][:, j * 128:(j + 1) * 128], zt[zb][:, k * 128:(k + 1) * 128], ident_f), deps=deps)
                    dst = hTs[:, kg * 4:(kg + 1) * 4, tl * 128:(tl + 1) * 128]
                    src = banks[bk][:].rearrange("p (a b) -> p a b", a=4)
                    ev = P.add("act", lambda e, dst=dst, src=src: e.activation(out=dst, in_=src, func=AF.Copy),
                               deps=[trl, hs["hTs_free"]])
                    tr_bank_free[bk] = ev
                    hs["hT_evs"].append(ev)
                zt_free[zb] = [dm, trl]
                if tl == 3:
                    hd = P.add("sp", lambda e, gi=gi: e.dma_start(
                        out=hT_s[:, :, gi * 512:(gi + 1) * 512].rearrange("k p t -> p k t"), in_=hTs),
                        deps=[hs["hT_evs"][-1]], dsem="hTs")
                    hs["hTs_free"] = hd
                    tails.append(hd)

            load_a(0)
            load_r(0)
            for it in range(32 + 2):
                if it < 32:
                    stage0(it)
                if 0 <= it - 1 < 32:
                    stage1(it - 1)
                if 0 <= it - 2 < 32:
                    stage2(it - 2)
            set_barrier(tails[-4:] + [P.q["act"][-1], P.q["dve"][-1], P.q["pe"][-1], P.q["pool"][-1]])
            return tails

        if "C" in phases:
            s2_phase("C", 16, wout_b, 0, yT_s, x, ALPHA, ln1_g, ln1_b, "C", h_s, 1, wres_first=True,
                     prefetch_w=(16 if FUSE else 0))

        def phaseD_all():
            A = Alloc(BASE)
            hTb = [A.get([128, KC, HALF], BF16) for _ in range(2)]
            wbufs = [A.get([128, KC, 128], BF16) for _ in range(3)]
            raw = [A.get([128, HALF + 2], F32) for _ in range(2)]
            aconv = [A.get([128, HALF], F32) for _ in range(2)]
            gconv = [A.get([128, HALF], F32) for _ in range(2)]
            astg = [A.get([128, HALF], BF16) for _ in range(2)]
            pre = barrier_ops()
            hT_k = []
            hT1_all = []
            for hf in range(2):
                for k in range(KC):
                    ldh = P.add("sp", lambda e, k=k, hf=hf: e.dma_start(
                        out=hTb[hf][:, k, :], in_=hT_s[k, :, hf * HALF:(hf + 1) * HALF]), deps=pre,
                        dsem=("hT0_%d" % k) if hf == 0 else "hT1")
                    if hf == 0:
                        hT_k.append(ldh)
                    else:
                        hT1_all = [ldh]
            st = {"raw_free": [pre, pre], "i": 0, "aconv_free": [pre, pre], "gconv_free": [pre, pre],
                  "astg_dma": [pre, pre], "aconv_done": [None, None], "hf": 0}

            def epi(info, bset, mm_last):
                hf = st["hf"]
                t0 = hf * HALF
                kind, cb = info
                ch = cb if kind == "a" else NFB + cb
                ri = st["i"] % 2
                st["i"] += 1
                ci = cb % 2
                readers = []
                tails = []
                if hf == 0:
                    hop = P.add("dve", lambda e, ri=ri: e.memset(raw[ri][:, 0:2], 0.0), deps=[st["raw_free"][ri]])
                else:
                    hop = P.add("dve", lambda e, ri=ri, ch=ch: e.tensor_copy(out=raw[ri][:, 0:2], in_=halo_f[:, ch, :]),
                                deps=[st["raw_free"][ri]])
                for n in range(4):
                    dst = raw[ri][:, 2 + n * 512:2 + (n + 1) * 512]
                    ev = P.add("act", lambda e, dst=dst, bk=bset * 4 + n: e.activation(
                        out=dst, in_=banks[bk][:], func=AF.Copy), deps=[mm_last[n], st["raw_free"][ri]])
                    readers.append(ev)
                rdone = [readers[-1], hop]
                if hf == 0:
                    hs = P.add("dve", lambda e, ri=ri, ch=ch: e.tensor_copy(
                        out=halo_f[:, ch, :], in_=raw[ri][:, HALF:HALF + 2]), deps=rdone)
                    tails.append(hs)
                dstb = aconv[ci] if kind == "a" else gconv[ci]
                dfree = st["aconv_free"][ci] if kind == "a" else st["gconv_free"][ci]
                c1 = P.add("dve", lambda e, ri=ri, ch=ch, dstb=dstb: e.tensor_scalar(
                    out=dstb, in0=raw[ri][:, 2:HALF + 2], scalar1=fcw[:, 2, ch:ch + 1], scalar2=fcb[:, ch:ch + 1],
                    op0=ALU.mult, op1=ALU.add), deps=rdone + [dfree, ld_fcw, ld_fcb])
                c2 = P.add("dve", lambda e, ri=ri, ch=ch, dstb=dstb: e.scalar_tensor_tensor(
                    out=dstb, in0=raw[ri][:, 1:HALF + 1], scalar=fcw[:, 1, ch:ch + 1], in1=dstb,
                    op0=ALU.mult, op1=ALU.add), deps=[c1])
                c3 = P.add("dve", lambda e, ri=ri, ch=ch, dstb=dstb: e.scalar_tensor_tensor(
                    out=dstb, in0=raw[ri][:, 0:HALF], scalar=fcw[:, 0, ch:ch + 1], in1=dstb,
                    op0=ALU.mult, op1=ALU.add), deps=[c2])
                st["raw_free"][ri] = c3
                if kind == "a":
                    st["aconv_done"][ci] = c3
                else:
                    sg = P.add("act", lambda e, ci=ci: e.activation(out=gconv[ci], in_=gconv[ci], func=AF.Silu),
                               deps=[c3])
                    mu = P.add("pool", lambda e, ci=ci: e.tensor_tensor(
                        out=astg[ci], in0=gconv[ci], in1=aconv[ci], op=ALU.mult),
                        deps=[sg, st["aconv_done"][ci], st["astg_dma"][ci]])
                    st["aconv_free"][ci] = mu
                    st["gconv_free"][ci] = mu
                    dm = P.add("sp", lambda e, ci=ci, cb=cb, t0=t0: e.dma_start(
                        out=actT_s[cb, :, t0:t0 + HALF], in_=astg[ci]), deps=[mu], dsem="astg%d" % ci)
                    st["astg_dma"][ci] = dm
                    tails.append(dm)
                return readers, tails

            blocks = []
            for cb in range(NFB):
                blocks.append((cb * 128, ("a", cb)))
                blocks.append((DFF + cb * 128, ("g", cb)))
            tail0 = s1_half("D0", hTb[0], None, w_up, blocks, epi, wbufs, pre, aT_ready_k=hT_k,
                            next_cols=[blocks[0][0], blocks[1][0]])
            pf_deps = [o for o in tail0[-3:] if o is not None]
            wres_e = carve(BASE, [128, 22, D], BF16)
            pf = None
            for kk in range(0, 16, 4):
                pf = P.add("sp", lambda e, kk=kk: e.dma_start(
                    out=wres_e[:, kk:kk + 4, :],
                    in_=wdown_b[kk * 128:(kk + 4) * 128, :].rearrange("(k p) n -> p k n", p=128)),
                    deps=pf_deps + state["precast_ops"][-1:], dsem="wres")
            state["w_pref"] = pf
            st["hf"] = 1
            pre1 = [o for o in tail0 if o is not None] + [P.q["act"][-1], P.q["dve"][-1], P.q["pe"][-1], P.q["pool"][-1]]
            tail1 = s1_half("D1", hTb[1], hT1_all, w_up, blocks, epi, wbufs, pre1, pre_load=pf_deps,
                            bank_free_init=state["last_banks"], slot_off=len(blocks) % 3, cont=True)
            set_barrier(tail1 + [P.q["act"][-1], P.q["dve"][-1], P.q["pe"][-1], P.q["pool"][-1]])

        if "D" in phases:
            phaseD_all()

        if "E" in phases:
            s2_phase("E1", 22, wdown_b, 0, actT_s, h_s, ALPHA, None, None, "E1", part_s, 2, wres_first=True,
                     prefetch_w=(16 if "D" in phases else 0))
            tails = s2_phase("E2", 22, wdown_b, 22, actT_s, part_s, None, ln2_g, ln2_b, "E2", out, 2, wres_first=True)

        P.finals = [o for o in state["bar"] if o is not None and o.dsem is not None]
        last_c = [o for o in state["bar"] if o is not None and o.dsem is None]
        if last_c:
            fin = P.add("sp", lambda e: e.nop(), deps=last_c)
        esems = {e: es.enter_context(nc.semaphore("s_" + e)) for e in ENGS}
        dsems = {n: es.enter_context(nc.semaphore("d_" + n)) for n in P.dsem_names()}
        block = es.enter_context(nc.Block())
        P.emit(block, esems, dsems)
    return nc


def _t5_bucket(dist):
    max_exact = 16
    n = np.maximum(dist, 1).astype(np.float32)
    large = max_exact + (np.log(n / max_exact) / math.log(2048 / max_exact) * (32 - max_exact)).astype(np.int32)
    large = np.minimum(large, 31)
    return np.where(dist < max_exact, dist, large).astype(np.int32)


def _bias_tables(rel_bias):
    rel_bias = np.asarray(rel_bias, dtype=np.float32)
    k = np.arange(128)[:, None]
    qq = np.arange(256)[None, :]
    dist = qq - k
    valid = (dist >= 0) & (dist <= 128)
    outp = np.full((24, 128, 256), MASKV, dtype=np.float32)
    for g, d in enumerate(DILS):
        bucket = _t5_bucket(np.clip(dist, 0, None) * d)
        for h in range(8):
            col = rel_bias[:, g * 8 + h]
            t = col[bucket]
            outp[g * 8 + h][valid] = t[valid]
    return outp


_NC_CACHE = {}


def _get_nc(phases="0ABCDE", dbg=False):
    key = (phases, dbg)
    if key not in _NC_CACHE:
        _NC_CACHE[key] = build_nc(phases, dbg)
    return _NC_CACHE[key]


def make_in_maps(inputs, ncores=8):
    f = lambda a: np.ascontiguousarray(np.asarray(a, dtype=np.float32))
    x = f(inputs["x"])
    shared = {
        "w_in": f(inputs["w_in"])[0], "conv_mix_w": f(inputs["conv_mix_w"])[0], "w_out": f(inputs["w_out"])[0],
        "ln1_g": f(inputs["ln1_g"]), "ln1_b": f(inputs["ln1_b"]), "w_up": f(inputs["w_up"])[0],
        "ffn_conv_w": f(inputs["ffn_conv_w"])[0], "ffn_conv_b": f(inputs["ffn_conv_b"]),
        "w_down": f(inputs["w_down"])[0], "ln2_g": f(inputs["ln2_g"]), "ln2_b": f(inputs["ln2_b"]),
        "biasT": _bias_tables(inputs["rel_bias"]), "ident": np.eye(128, dtype=np.float32),
    }
    maps = []
    for c in range(ncores):
        m = dict(shared)
        m["x"] = np.ascontiguousarray(x[c])
        maps.append(m)
    return maps


def kernel(**inputs):
    nc = _get_nc()
    in_maps = make_in_maps(inputs, 8)
    res = run_bass_kernel_spmd(nc, in_maps, core_ids=list(range(8)))
    return np.stack([np.asarray(r["out"], dtype=np.float32) for r in res.results], axis=0)
```
